# Optimizing a Trainium2 kernel written in Bass

```python
import jax, jax.numpy as jnp
from jax import lax
import numpy as np

D_MODEL = 1024
BATCH = 2
SEQ = 8192
DEPTH = 2
DEC_BATCH = 8
DEC_SEQ = 8192
PAST_LEN = 128

N_META = 16
GRID_W = 64
EPS = 1e-6
POOL_WINDOWS = (2, 4, 8, 16)
POOL_WIDTH = D_MODEL // 2
POOL_GROUP = POOL_WIDTH // len(POOL_WINDOWS)
DN_DK = 128
DN_DV = 128
DN_HEADS = (D_MODEL // 2) // DN_DV
DN_CONV = 7
CHUNK = 64
DN_QK = DN_HEADS * DN_DK
DN_V = DN_HEADS * DN_DV
EVEN_IN = POOL_WIDTH + 2 * DN_QK + 2 * DN_V + 4 * DN_HEADS
EVEN_OUT = POOL_WIDTH + DN_V
ATT_HD = 128
ATT_HEADS = D_MODEL // ATT_HD
ATT_KV_HEADS = ATT_HEADS // 4
ATT_GROUP = ATT_HEADS // ATT_KV_HEADS
ROPE_THETA = 10000.0
ROPE_PAIRS_AXIS = ATT_HD // 4
Q_BLOCK = 128
ODD_IN = (ATT_HEADS + 2 * ATT_KV_HEADS) * ATT_HD
D_FF = 4 * D_MODEL
N_EVEN = (DEPTH + 1) // 2
N_ODD = DEPTH // 2

kernel_name = 'hybrid_pool_deltanet_gqa_encoder'

F32 = jnp.float32


def rms_norm(x, g):
    xf = x.astype(F32)
    y = xf * lax.rsqrt(jnp.mean(xf * xf, axis=-1, keepdims=True) + EPS)
    return (y * g.astype(F32)).astype(x.dtype)


def l2_norm(x):
    xf = x.astype(F32)
    return xf * lax.rsqrt(jnp.sum(xf * xf, axis=-1, keepdims=True) + EPS)


def pool_mixer(u, pool_w, pool_scale):
    bsz, n, _ = u.shape
    uf = u.astype(F32)
    cs = jnp.concatenate([jnp.zeros((bsz, 1, POOL_WIDTH), F32), jnp.cumsum(uf, axis=1)], axis=1)
    t = jnp.arange(n)
    means = []
    for gi, w in enumerate(POOL_WINDOWS):
        lo = jnp.clip(t - w // 2, 0, n - 1)
        hi = jnp.clip(t + (w - 1 - w // 2), 0, n - 1)
        csg = cs[..., gi * POOL_GROUP:(gi + 1) * POOL_GROUP]
        s = jnp.take(csg, hi + 1, axis=1) - jnp.take(csg, lo, axis=1)
        means.append(s / (hi - lo + 1).astype(F32)[None, :, None])
    d = (jnp.concatenate(means, axis=-1) - uf).astype(u.dtype)
    d = d.reshape(bsz, n, len(POOL_WINDOWS), POOL_GROUP)
    y = jnp.einsum('blgc,gcd->blgd', d, pool_w).reshape(bsz, n, POOL_WIDTH)
    return y * pool_scale


def gated_delta_chunked(q, k, v, beta, g):
    nb, t, h, dk = q.shape
    dv = v.shape[-1]
    nc = t // CHUNK

    def blk(a):
        a = a.astype(F32).reshape((nb, nc, CHUNK, h) + a.shape[3:])
        return jnp.moveaxis(a, 3, 1)

    q, k, v, beta, g = blk(q), blk(k), blk(v), blk(beta), blk(g)
    gc = jnp.cumsum(g, axis=-1)
    idx = jnp.arange(CHUNK)
    lower = idx[:, None] >= idx[None, :]
    strict = idx[:, None] > idx[None, :]
    decay = jnp.exp(jnp.where(lower, gc[..., :, None] - gc[..., None, :], -jnp.inf))
    kb = k * beta[..., None]
    lmat = jnp.where(strict, jnp.einsum('nhcid,nhcjd->nhcij', kb, k) * decay, 0.0)
    a_mat = lmat + jnp.eye(CHUNK, dtype=F32)
    u_c = lax.linalg.triangular_solve(a_mat, v * beta[..., None], left_side=True, lower=True, unit_diagonal=True)
    w_c = lax.linalg.triangular_solve(a_mat, kb * jnp.exp(gc)[..., None], left_side=True, lower=True, unit_diagonal=True)
    attn = jnp.einsum('nhcid,nhcjd->nhcij', q, k) * decay
    g_last = gc[..., -1]
    k_dec = k * jnp.exp(g_last[..., None] - gc)[..., None]
    q_dec = q * jnp.exp(gc)[..., None]
    xs = tuple(jnp.moveaxis(a, 2, 0) for a in (q_dec, k_dec, u_c, w_c, attn, g_last))

    def step(s, inp):
        qd, kd, uc, wc, at, gl = inp
        v_new = uc - jnp.einsum('nhcd,nhde->nhce', wc, s)
        o = jnp.einsum('nhcd,nhde->nhce', qd, s) + jnp.einsum('nhij,nhje->nhie', at, v_new)
        s = s * jnp.exp(gl)[..., None, None] + jnp.einsum('nhcd,nhce->nhde', kd, v_new)
        return s, o

    s0 = jnp.zeros((nb, h, dk, dv), F32)
    _, o = lax.scan(step, s0, xs)
    return jnp.transpose(o, (1, 0, 3, 2, 4)).reshape(nb, t, h, dv)


def delta_mixer(u_qkv, z, b, a, conv_w, a_log, dt_bias, norm_g):
    bsz, n, c = u_qkv.shape
    qkv = lax.conv_general_dilated(u_qkv, conv_w[:, None, :], (1,), [(DN_CONV // 2, DN_CONV // 2)],
                                   dimension_numbers=('NWC', 'WIO', 'NWC'), feature_group_count=c)
    qkv = jax.nn.silu(qkv)
    q = l2_norm(qkv[..., :DN_QK].reshape(bsz, n, DN_HEADS, DN_DK)) * (DN_DK ** -0.5)
    k = l2_norm(qkv[..., DN_QK:2 * DN_QK].reshape(bsz, n, DN_HEADS, DN_DK))
    v = qkv[..., 2 * DN_QK:].astype(F32).reshape(bsz, n, DN_HEADS, DN_DV)
    beta = jax.nn.sigmoid(b.astype(F32)).reshape(bsz, n, 2, DN_HEADS)
    g = -jnp.exp(a_log.astype(F32)) * jax.nn.softplus(a.astype(F32).reshape(bsz, n, 2, DN_HEADS) + dt_bias.astype(F32))
    pad = (-n) % CHUNK

    def pad_t(x, front):
        widths = [(0, 0), (pad, 0) if front else (0, pad)] + [(0, 0)] * (x.ndim - 2)
        return jnp.pad(x, widths)

    def both(xf, xb):
        return jnp.concatenate([pad_t(xf, True), pad_t(jnp.flip(xb, axis=1), False)], axis=0)

    o = gated_delta_chunked(both(q, q), both(k, k), both(v, v),
                            both(beta[:, :, 0], beta[:, :, 1]), both(g[:, :, 0], g[:, :, 1]))
    o = o[:bsz, pad:] + jnp.flip(o[bsz:, :n], axis=1)
    o = rms_norm(o, norm_g) * jax.nn.silu(z.astype(F32).reshape(bsz, n, DN_HEADS, DN_DV))
    return o.reshape(bsz, n, DN_V).astype(u_qkv.dtype)


def rope_tables(n_tokens):
    rows = n_tokens // GRID_W
    row = jnp.repeat(jnp.arange(rows), GRID_W).astype(F32)
    col = jnp.tile(jnp.arange(GRID_W), rows).astype(F32)
    freqs = ROPE_THETA ** (-(jnp.arange(ROPE_PAIRS_AXIS, dtype=F32) / ROPE_PAIRS_AXIS))
    ang = jnp.concatenate([row[:, None] * freqs, col[:, None] * freqs], axis=-1)
    ang = jnp.concatenate([jnp.zeros((N_META, 2 * ROPE_PAIRS_AXIS), F32), ang], axis=0)
    return jnp.cos(ang), jnp.sin(ang)


def apply_rope(x, cos, sin):
    half = x.shape[-1] // 2
    shp = (x.shape[1],) + (1,) * (x.ndim - 3) + (half,)
    c, s = cos.reshape(shp), sin.reshape(shp)
    xf = x.astype(F32)
    x1, x2 = xf[..., :half], xf[..., half:]
    return jnp.concatenate([x1 * c - x2 * s, x2 * c + x1 * s], axis=-1).astype(x.dtype)


def attention_mixer(u, q_norm, k_norm, cos, sin):
    bsz, n, _ = u.shape
    hq, hk = ATT_HEADS * ATT_HD, ATT_KV_HEADS * ATT_HD
    q = u[..., :hq].reshape(bsz, n, ATT_KV_HEADS, ATT_GROUP, ATT_HD)
    k = u[..., hq:hq + hk].reshape(bsz, n, ATT_KV_HEADS, ATT_HD)
    v = u[..., hq + hk:].reshape(bsz, n, ATT_KV_HEADS, ATT_HD)
    q = apply_rope(rms_norm(q, q_norm), cos, sin)
    k = apply_rope(rms_norm(k, k_norm), cos, sin)
    scale = ATT_HD ** -0.5

    def attend(qb):
        s = jnp.einsum('bqhgd,bkhd->bhgqk', qb, k).astype(F32) * scale
        p = jax.nn.softmax(s, axis=-1)
        return jnp.einsum('bhgqk,bkhd->bqhgd', p.astype(v.dtype), v)

    out_meta = attend(q[:, :N_META])
    n_real = n - N_META
    qr = jnp.moveaxis(q[:, N_META:].reshape(bsz, n_real // Q_BLOCK, Q_BLOCK, ATT_KV_HEADS, ATT_GROUP, ATT_HD), 1, 0)
    out_real = jnp.moveaxis(lax.map(attend, qr), 0, 1).reshape(bsz, n_real, ATT_KV_HEADS, ATT_GROUP, ATT_HD)
    return jnp.concatenate([out_meta, out_real], axis=1).reshape(bsz, n, hq)


def trunk(x, p):
    bsz, s, _ = x.shape
    h = jnp.concatenate([jnp.broadcast_to(p['meta_tokens'][None].astype(x.dtype), (bsz, N_META, D_MODEL)), x], axis=1)
    cos, sin = rope_tables(s)
    o1 = POOL_WIDTH
    o2 = o1 + 2 * DN_QK + DN_V
    o3 = o2 + DN_V
    o4 = o3 + 2 * DN_HEADS
    for i in range(DEPTH):
        xn = rms_norm(h, p['mix_norm'][i])
        if i % 2 == 0:
            j = i // 2
            u = xn @ p['w_in_even'][j]
            y_pool = pool_mixer(u[..., :o1], p['pool_w'][j], p['pool_scale'][j])
            y_dn = delta_mixer(u[..., o1:o2], u[..., o2:o3], u[..., o3:o4], u[..., o4:],
                               p['conv_qkv'][j], p['a_log'][j], p['dt_bias'][j], p['delta_norm'][j])
            mix = jnp.concatenate([y_pool.astype(h.dtype), y_dn.astype(h.dtype)], axis=-1) @ p['w_out_even'][j]
        else:
            j = i // 2
            u = xn @ p['w_in_odd'][j]
            mix = attention_mixer(u, p['q_norm'][j], p['k_norm'][j], cos, sin) @ p['w_out_odd'][j]
        h = h + mix.astype(h.dtype)
        hn = rms_norm(h, p['mlp_norm'][i])
        h = h + (jnp.square(jax.nn.relu(hn @ p['w_mlp_in'][i])) @ p['w_mlp_out'][i]).astype(h.dtype)
    return h[:, N_META:]


def setup_inputs(seed: int = 0) -> dict:
    key = jax.random.key(seed)
    ks = jax.random.split(key, 20)
    nrm = jax.random.normal
    dt = jnp.exp(jax.random.uniform(ks[10], (N_EVEN, 2, DN_HEADS), F32, np.log(1e-3), np.log(1e-1)))
    return {
        'x_prompt': nrm(ks[0], (BATCH, SEQ, D_MODEL), F32),
        'x_sample': nrm(ks[1], (DEC_BATCH, DEC_SEQ, D_MODEL), F32),
        'meta_tokens': nrm(ks[2], (N_META, D_MODEL), F32),
        'mix_norm': 1.0 + 0.02 * nrm(ks[3], (DEPTH, D_MODEL), F32),
        'mlp_norm': 1.0 + 0.02 * nrm(ks[4], (DEPTH, D_MODEL), F32),
        'w_in_even': nrm(ks[5], (N_EVEN, D_MODEL, EVEN_IN), F32) * D_MODEL ** -0.5,
        'pool_w': nrm(ks[6], (N_EVEN, len(POOL_WINDOWS), POOL_GROUP, POOL_GROUP), F32) * POOL_GROUP ** -0.5,
        'pool_scale': 1.0 + 0.1 * nrm(ks[7], (N_EVEN, POOL_WIDTH), F32),
        'conv_qkv': nrm(ks[8], (N_EVEN, DN_CONV, 2 * DN_QK + DN_V), F32) * DN_CONV ** -0.5,
        'a_log': jnp.log(jax.random.uniform(ks[9], (N_EVEN, 2, DN_HEADS), F32, 1.0, 16.0)),
        'dt_bias': dt + jnp.log(-jnp.expm1(-dt)),
        'delta_norm': 1.0 + 0.02 * nrm(ks[11], (N_EVEN, DN_DV), F32),
        'w_out_even': nrm(ks[12], (N_EVEN, EVEN_OUT, D_MODEL), F32) * EVEN_OUT ** -0.5,
        'w_in_odd': nrm(ks[13], (N_ODD, D_MODEL, ODD_IN), F32) * D_MODEL ** -0.5,
        'q_norm': 1.0 + 0.02 * nrm(ks[14], (N_ODD, ATT_HD), F32),
        'k_norm': 1.0 + 0.02 * nrm(ks[15], (N_ODD, ATT_HD), F32),
        'w_out_odd': nrm(ks[16], (N_ODD, ATT_HEADS * ATT_HD, D_MODEL), F32) * (ATT_HEADS * ATT_HD) ** -0.5,
        'w_mlp_in': nrm(ks[17], (DEPTH, D_MODEL, D_FF), F32) * D_MODEL ** -0.5,
        'w_mlp_out': nrm(ks[18], (DEPTH, D_FF, D_MODEL), F32) * D_FF ** -0.5,
    }


def reference(x_prompt, x_sample, meta_tokens, mix_norm, mlp_norm, w_in_even, pool_w, pool_scale, conv_qkv,
              a_log, dt_bias, delta_norm, w_out_even, w_in_odd, q_norm, k_norm, w_out_odd, w_mlp_in, w_mlp_out):
    params = {
        'meta_tokens': meta_tokens, 'mix_norm': mix_norm, 'mlp_norm': mlp_norm,
        'w_in_even': w_in_even, 'pool_w': pool_w, 'pool_scale': pool_scale, 'conv_qkv': conv_qkv,
        'a_log': a_log, 'dt_bias': dt_bias, 'delta_norm': delta_norm, 'w_out_even': w_out_even,
        'w_in_odd': w_in_odd, 'q_norm': q_norm, 'k_norm': k_norm, 'w_out_odd': w_out_odd,
        'w_mlp_in': w_mlp_in, 'w_mlp_out': w_mlp_out,
    }
    y_prompt = trunk(x_prompt, params)
    y_sample = trunk(x_sample, params)
    return (y_prompt, y_sample)
```

```python
import numpy as np
from contextlib import ExitStack
import concourse.bass as bass
import concourse.mybir as mybir
from concourse.bass_utils import run_bass_kernel_spmd

F32 = mybir.dt.float32
BF16 = mybir.dt.bfloat16
AF = mybir.ActivationFunctionType
ALU = mybir.AluOpType

D = 1024
NMETA = 16
EPS = 1e-6
EVEN_IN = 2576
ODD_IN = 1536
DFF = 4096
NEG = -30000.0


class Buf:
    def __init__(self, t, name):
        self.t = t
        self.name = name
        self.lastw = {}
        self.wkind = None
        self.readers = {}
        self.psum = False

    def __getitem__(self, idx):
        return self.t[idx]


class Eng:
    def __init__(self, name, h, sem):
        self.name = name
        self.h = h
        self.sem = sem
        self.count = 0
        self.waited = {}
        self.dsems = []
        self.dnext = 0


NDSEM = 16


class KB:
    def __init__(self, nc, stack):
        self.nc = nc
        self.stack = stack
        self.phase = None
        self.e = {}
        for name, h in (("pe", nc.tensor), ("act", nc.scalar), ("dve", nc.vector), ("pool", nc.gpsimd), ("sp", nc.sync)):
            sem = stack.enter_context(nc.semaphore("s_" + name))
            self.e[name] = Eng(name, h, sem)
        for q in ("sp", "pool"):
            self.e[q].dsems = [[stack.enter_context(nc.semaphore("d_%s%d" % (q, i))), 0] for i in range(NDSEM)]
        self.nbuf = 0
        self.ninst = 0

    def push_phase(self):
        self.phase = ExitStack()
        return self.phase

    def sb(self, shape, dtype, name=None, persistent=False):
        self.nbuf += 1
        name = (name or "b") + "_%d" % self.nbuf
        st = self.stack if persistent else self.phase
        t = st.enter_context(self.nc.sbuf_tensor(name, list(shape), dtype))
        return Buf(t, name)

    def ps(self, shape, dtype, name=None):
        self.nbuf += 1
        name = (name or "p") + "_%d" % self.nbuf
        t = self.phase.enter_context(self.nc.psum_tensor(name, list(shape), dtype))
        b = Buf(t, name)
        b.psum = True
        return b

    def dram(self, name, shape, dtype, kind="Internal"):
        t = self.nc.dram_tensor(name, list(shape), dtype, kind=kind).ap()
        return Buf(t, name)

    def _wait(self, eng, deps):
        need = {}
        for s, v in deps:
            if need.get(s, 0) < v:
                need[s] = v
        for s, v in need.items():
            if eng.waited.get(s, 0) >= v:
                continue
            eng.h.wait_ge(s, v)
            eng.waited[s] = v

    def op(self, en, fns, reads=(), writes=()):
        eng = self.e[en]
        deps = []
        for b in reads:
            for s, v in b.lastw.items():
                if not (en == "pe" and s is eng.sem):
                    deps.append((s, v))
            if b.psum:
                for s, v in b.readers.items():
                    if s is not eng.sem:
                        deps.append((s, v))
        for b in writes:
            for s, v in b.lastw.items():
                if not (en == "pe" and s is eng.sem):
                    deps.append((s, v))
            for s, v in b.readers.items():
                if not (en == "pe" and s is eng.sem):
                    deps.append((s, v))
        self._wait(eng, deps)
        if not isinstance(fns, (list, tuple)):
            fns = [fns]
        ins = None
        for f in fns:
            ins = f(eng.h)
            self.ninst += 1
        eng.count += 1
        ins.then_inc(eng.sem, 1)
        for b in reads:
            if b.readers.get(eng.sem, 0) < eng.count:
                b.readers[eng.sem] = eng.count
        for b in writes:
            b.lastw = {eng.sem: eng.count}
            b.wkind = "eng"
            b.readers = {}

    def dma(self, q, out_ap, in_ap, src, dst, **kw):
        eng = self.e[q]
        slot = eng.dsems[eng.dnext % NDSEM]
        eng.dnext += 1
        sem, prev = slot
        deps = list(src.lastw.items())
        parallel = dst.wkind == "dma" and not dst.readers
        if not parallel:
            deps.extend(dst.lastw.items())
            deps.extend(dst.readers.items())
        if prev:
            deps.append((sem, prev))
        self._wait(eng, deps)
        ins = eng.h.dma_start(out=out_ap, in_=in_ap, **kw)
        self.ninst += 1
        val = prev + 16
        slot[1] = val
        ins.then_inc(sem, 16)
        if src.readers.get(sem, 0) < val:
            src.readers[sem] = val
        if parallel:
            dst.lastw[sem] = val
        else:
            dst.lastw = {sem: val}
        dst.wkind = "dma"
        dst.readers = {}

    def barrier(self, bufs):
        deps = []
        for b in bufs:
            deps.extend(b.lastw.items())
            deps.extend(b.readers.items())
        for eng in self.e.values():
            if eng.count:
                deps.append((eng.sem, eng.count))
            for sem, val in eng.dsems:
                if val:
                    deps.append((sem, val))
        for eng in self.e.values():
            self._wait(eng, deps)


def mm(out, lhsT, rhs, start=True, stop=True):
    return lambda e: e.matmul(out, lhsT, rhs, start=start, stop=stop)


def tr(out, in_, ident):
    return lambda e: e.transpose(out, in_, ident)


def act(out, in_, func, bias=None, scale=None, accum_out=None):
    kw = {}
    if bias is not None:
        kw["bias"] = bias
    if scale is not None:
        kw["scale"] = scale
    if accum_out is not None:
        kw["accum_out"] = accum_out
    return lambda e: e.activation(out, in_, func, **kw)


def tt(out, a, b, op):
    return lambda e: e.tensor_tensor(out, a, b, op)


def tsc(out, a, s1, op0, s2=None, op1=None):
    if op1 is None:
        return lambda e: e.tensor_scalar(out, a, s1, None, op0)
    return lambda e: e.tensor_scalar(out, a, s1, s2, op0, op1)


def stt(out, a, s, b, op0, op1):
    return lambda e: e.scalar_tensor_tensor(out, a, s, b, op0, op1)


def cp(out, in_):
    return lambda e: e.tensor_copy(out, in_)


def acp(out, in_):
    return lambda e: e.activation(out, in_, AF.Copy)


def rcp(out, in_):
    return lambda e: e.reciprocal(out, in_)


def mset(ap, v):
    return lambda e: e.memset(ap, v)


class Prog:
    def __init__(self, S, NSLOT, debug=False, stop_after=None):
        self.S = S
        self.L = S + NMETA
        self.NSLOT = NSLOT
        self.debug = debug
        self.stop_after = stop_after
        self.nc = bass.Bass("TRN2", target_bir_lowering=False)
        self.stack = ExitStack()
        self.k = KB(self.nc, self.stack)
        self.tiles = [(0, NMETA)] + [(NMETA + 512 * i, 512) for i in range(S // 512)]
        self.chunks = [(0, NMETA)] + [(NMETA + 128 * i, 128) for i in range(S // 128)]

    def declare(self):
        k, L, S = self.k, self.L, self.S
        ext = lambda n, s, dt=F32: k.dram(n, s, dt, kind="ExternalInput")
        self.x = [ext("x%d" % s, [S, D]) for s in range(self.NSLOT)]
        self.y = [k.dram("y%d" % s, [S, D], F32, kind="ExternalOutput") for s in range(self.NSLOT)]
        self.meta = ext("meta", [NMETA, D])
        self.g_mix = ext("g_mix", [128, 2, 8])
        self.g_mlp = ext("g_mlp", [128, 2, 8])
        self.w_in_even = ext("w_in_even", [D, EVEN_IN])
        self.pool_w = ext("pool_w", [4, 128, 128])
        self.pool_scale = ext("pool_scale", [128, 4])
        self.conv_w = ext("conv_w", [128, 12, 7])
        self.a_log = ext("a_log", [128, 8])
        self.dt_bias = ext("dt_bias", [128, 8])
        self.delta_norm = ext("delta_norm", [128, 1])
        self.w_out_even = ext("w_out_even", [D, D])
        self.w_in_odd = ext("w_in_odd", [D, ODD_IN])
        self.qk_norm = ext("qk_norm", [128, 2])
        self.w_out_odd = ext("w_out_odd", [D, D])
        self.w_mlp_in = ext("w_mlp_in", [2, D, DFF])
        self.w_mlp_out = ext("w_mlp_out", [2, DFF, D])
        self.c_ident = ext("c_ident", [128, 128])
        self.c_masks = ext("c_masks", [128, 6, 128])
        self.c_perm = ext("c_perm", [128, 128])
        self.c_rope = ext("c_rope", [2, 128, L])
        self.wt_in_even = k.dram("wt_in_even", [21, 128, 8 * 128], BF16)
        self.wt_out_even = k.dram("wt_out_even", [8, 128, 8 * 128], BF16)
        self.wt_in_odd = k.dram("wt_in_odd", [12, 128, 8 * 128], BF16)
        self.wt_out_odd = k.dram("wt_out_odd", [8, 128, 8 * 128], BF16)
        self.wt_mlp_in = [k.dram("wt_mlp_in%d" % i, [32, 128, 8 * 128], BF16) for i in range(2)]
        self.wt_mlp_out = [k.dram("wt_mlp_out%d" % i, [8, 128, 32 * 128], BF16) for i in range(2)]
        dbg = "ExternalOutput" if self.debug else "Internal"
        self.H0 = k.dram("H0", [8, 128, L], F32, kind=dbg)
        self.UP = k.dram("UP", [4, 128, L], F32, kind=dbg)
        self.UQ = k.dram("UQ", [12, 128, L], BF16, kind=dbg)
        self.UZ = k.dram("UZ", [4, 128, L], F32, kind=dbg)
        self.UBA = k.dram("UBA", [L, 16], F32, kind=dbg)
        self.QKVN = k.dram("QKVN", [12, 128, L], BF16, kind=dbg)
        self.YM = k.dram("YM", [8, 128, L], BF16, kind=dbg)
        self.OF = k.dram("OF", [2, 4, 128, L], F32, kind=dbg)
        self.QT = k.dram("QT", [8, 128, L], BF16, kind=dbg)
        self.KT = k.dram("KT", [2, 128, L], BF16, kind=dbg)
        self.VT = k.dram("VT", [L, 256], BF16, kind=dbg)
        self.AT = k.dram("AT", [8, 128, L], BF16, kind=dbg)
        self.H1 = k.dram("H1", [8, 128, L], F32, kind=dbg)

    def load_consts(self):
        k = self.k
        P = True
        self.ident = k.sb([128, 128], F32, "ident", P)
        self.identb = k.sb([128, 128], BF16, "identb", P)
        self.onesb = k.sb([128, 128], BF16, "onesb", P)
        self.gmix = k.sb([128, 2, 8], F32, "gmix", P)
        self.gmlp = k.sb([128, 2, 8], F32, "gmlp", P)
        self.epsc = k.sb([128, 1], F32, "epsc", P)
        self.eps128 = k.sb([128, 1], F32, "eps128", P)
        self.onec = k.sb([128, 1], F32, "onec", P)
        k.dma("sp", self.ident[:], self.c_ident[:, :], self.c_ident, self.ident)
        k.dma("sp", self.gmix[:], self.g_mix[:, :, :], self.g_mix, self.gmix)
        k.dma("sp", self.gmlp[:], self.g_mlp[:, :, :], self.g_mlp, self.gmlp)
        k.op("dve", cp(self.identb[:], self.ident[:]), [self.ident], [self.identb])
        k.op("dve", mset(self.onesb[:], 1.0), [], [self.onesb])
        k.op("dve", mset(self.epsc[:], EPS), [], [self.epsc])
        k.op("dve", mset(self.eps128[:], 128.0 * EPS), [], [self.eps128])
        k.op("dve", mset(self.onec[:], 1.0), [], [self.onec])

    def _prolog_jobs(self):
        jobs = []
        def add(src, src2d, ncols, nk, gain, dst):
            for kc in range(nk):
                jobs.append((src, src2d, ncols, kc, gain, dst))
        add(self.w_in_even, self.w_in_even.t, EVEN_IN, 8, (self.gmix, 0), self.wt_in_even)
        add(self.w_out_even, self.w_out_even.t, D, 8, None, self.wt_out_even)
        add(self.w_in_odd, self.w_in_odd.t, ODD_IN, 8, (self.gmix, 1), self.wt_in_odd)
        add(self.w_out_odd, self.w_out_odd.t, D, 8, None, self.wt_out_odd)
        for i in range(2):
            add(self.w_mlp_in, self.w_mlp_in.t[i], DFF, 8, (self.gmlp, i), self.wt_mlp_in[i])
            add(self.w_mlp_out, self.w_mlp_out.t[i], D, 32, None, self.wt_mlp_out[i])
        return jobs

    def _prolog_run(self, jobs, stf, stb):
        k = self.k
        for n, (src, src2d, ncols, kc, gain, dst) in enumerate(jobs):
            f, b = stf[n % 2], stb[n % 2]
            k.dma("sp", f[:, 0:ncols], src2d[kc * 128:(kc + 1) * 128, :], src, f)
            nm = (ncols + 127) // 128
            nfull = ncols // 128
            if nm > nfull:
                k.op("pool", mset(b[:, ncols:nm * 128], 0.0), [], [b])
            if gain is not None:
                gb, gi = gain
                k.op("dve", tsc(b[:, 0:ncols], f[:, 0:ncols], gb[:, gi, kc:kc + 1], ALU.mult), [f, gb], [b])
            else:
                k.op("act", acp(b[:, 0:ncols], f[:, 0:ncols]), [f], [b])
            dv = dst.t
            k.dma("pool", dv[0:nfull, :, kc * 128:(kc + 1) * 128].rearrange("m p c -> p m c"),
                  b[:, 0:nfull * 128].rearrange("p (m c) -> p m c", c=128), b, dst)
            if nm > nfull:
                k.dma("pool", dv[nfull, :, kc * 128:(kc + 1) * 128], b[:, nfull * 128:nm * 128], b, dst)
            yield

    def prologue(self):
        k = self.k
        k.push_phase()
        with k.phase:
            stf = [k.sb([128, DFF], F32, "wstf") for _ in range(2)]
            stb = [k.sb([128, DFF], BF16, "wstb") for _ in range(2)]
            jobs = self._prolog_jobs()
            for _ in self._prolog_run(jobs[:8], stf, stb):
                pass
            self.pending_jobs = jobs[8:]
            k.barrier([self.wt_in_even])

    def rms_to_bf16(self, hT, w, sq, psS, tmp, rstd, xn, engs=("dve", "pool")):
        k = self.k
        if isinstance(sq, Buf):
            for c in range(8):
                k.op("act", act(sq[:, c, 0:w], hT[:, c, 0:w], AF.Square), [hT], [sq])
            k.op("pe", [mm(psS[:, 0:w], self.onesb[:], sq[:, c, 0:w], c == 0, c == 7) for c in range(8)], [sq, self.onesb], [psS])
        else:
            for c in range(8):
                s_ = sq[c % len(sq)]
                k.op("act", act(s_[:, 0:w], hT[:, c, 0:w], AF.Square), [hT], [s_])
                k.op("pe", mm(psS[:, 0:w], self.onesb[:], s_[:, 0:w], c == 0, c == 7), [s_, self.onesb], [psS])
        k.op("act", act(tmp[:, 0:w], psS[:, 0:w], AF.Ln, bias=self.epsc[:, 0:1], scale=1.0 / D), [psS, self.epsc], [tmp])
        k.op("act", act(rstd[:, 0:w], tmp[:, 0:w], AF.Exp, scale=-0.5), [tmp], [rstd])
        for c in range(8):
            k.op(engs[c % len(engs)], tt(xn[:, c, 0:w], hT[:, c, 0:w], rstd[:, 0:w], ALU.mult), [hT, rstd], [xn])

    def phase_a(self, slot):
        k = self.k
        k.push_phase()
        with k.phase:
            xt = [k.sb([128, 4, D], F32, "xt") for _ in range(2)]
            hT = [k.sb([128, 8, 512], F32, "hT") for _ in range(2)]
            sq = k.sb([128, 8, 512], BF16, "sq")
            tmp = k.sb([128, 512], F32, "tmp")
            rstd = k.sb([128, 512], F32, "rstd")
            xn = [k.sb([128, 8, 512], BF16, "xn") for _ in range(2)]
            wb = [k.sb([128, 3, 8 * 128], BF16, "wb") for _ in range(3)]
            of = [k.sb([128, 4, 512], F32, "of") for _ in range(2)]
            ob = [k.sb([128, 4, 512], BF16, "ob") for _ in range(2)]
            oba = [k.sb([128, 4, 16], F32, "oba") for _ in range(2)]
            psT = [k.ps([128, 512], F32, "psT") for _ in range(2)]
            psS = k.ps([128, 512], F32, "psS")
            psU = [k.ps([128, 512], F32, "psU") for _ in range(3)]
            psB = k.ps([128, 4, 16], F32, "psB")
            def T(ti):
                t0, w = self.tiles[ti]
                X, H, XN = xt[ti % 2], hT[ti % 2], xn[ti % 2]
                nsub = (w + 127) // 128
                if ti == 0:
                    k.dma("sp", X[0:NMETA, 0, :], self.meta[:, :], self.meta, X)
                else:
                    r0 = t0 - NMETA
                    k.dma("sp", X[:, :, :], self.x[slot].t[r0:r0 + 512, :].rearrange("(j p) d -> p j d", p=128), self.x[slot], X)
                pw = min(w, 128)
                for c in range(8):
                    pt = psT[c % 2]
                    k.op("pe", [tr(pt[:, j * 128:j * 128 + pw], X[0:pw, j, c * 128:(c + 1) * 128], self.ident[0:pw, 0:pw]) for j in range(nsub)],
                         [X, self.ident], [pt])
                    k.op("dve", cp(H[:, c, 0:w], pt[:, 0:w]), [pt], [H])
                self.rms_to_bf16(H, w, sq, psS, tmp, rstd, XN)
                k.dma("pool", self.H0.t[:, :, t0:t0 + w].rearrange("c p t -> p c t"), H[:, :, 0:w], H, self.H0)

            def P(ti):
                t0, w = self.tiles[ti]
                X, H, XN = xt[ti % 2], hT[ti % 2], xn[ti % 2]
                nsub = (w + 127) // 128
                pw = min(w, 128)
                for g in range(7):
                    W = wb[g % 3]
                    k.dma("sp", W[:, :, :], self.wt_in_even.t[3 * g:3 * g + 3, :, :].rearrange("m p c -> p m c"), self.wt_in_even, W)
                    for mi in range(3):
                        m = 3 * g + mi
                        if m < 20:
                            pu = psU[m % 3]
                            k.op("pe", [mm(pu[:, 0:w], W[:, mi, c * 128:(c + 1) * 128], XN[:, c, 0:w], c == 0, c == 7) for c in range(8)], [W, XN], [pu])
                            grp = m // 4
                            if grp == 0 or grp == 4:
                                o = of[0 if grp == 0 else 1]
                                dst, d0 = (self.UP, 0) if grp == 0 else (self.UZ, 0)
                            else:
                                o = ob[grp % 2]
                                dst, d0 = self.UQ, (grp - 1) * 4
                            k.op("act" if m % 2 else "dve", (acp if m % 2 else cp)(o[:, m % 4, 0:w], pu[:, 0:w]), [pu], [o])
                            if m % 4 == 3:
                                k.dma("pool", dst.t[d0:d0 + 4, :, t0:t0 + w].rearrange("g p t -> p g t"), o[:, :, 0:w], o, dst)
                        else:
                            O = oba[ti % 2]
                            k.op("pe", [mm(psB[0:pw, j, :], XN[:, c, j * 128:j * 128 + pw], W[:, mi, c * 128:c * 128 + 16], c == 0, c == 7)
                                        for j in range(nsub) for c in range(8)], [W, XN], [psB])
                            k.op("dve", cp(O[0:pw, 0:nsub, :], psB[0:pw, 0:nsub, :]), [psB], [O])
                            k.dma("pool", self.UBA.t[t0:t0 + w, :].rearrange("(j p) f -> p j f", p=pw), O[0:pw, 0:nsub, :], O, self.UBA)

            nt = len(self.tiles)
            pg, per = None, 0
            if self.pending_jobs:
                stf = [k.sb([128, DFF], F32, "wstf") for _ in range(2)]
                stb = [k.sb([128, DFF], BF16, "wstb") for _ in range(2)]
                pg = self._prolog_run(self.pending_jobs, stf, stb)
                per = (len(self.pending_jobs) + nt - 1) // nt
                self.pending_jobs = []
            T(0)
            for ti in range(nt):
                if ti + 1 < nt:
                    T(ti + 1)
                P(ti)
                if pg is not None:
                    for _ in range(per):
                        next(pg, None)
            if pg is not None:
                for _ in pg:
                    pass
            k.barrier([self.H0, self.UP, self.UQ, self.UZ, self.UBA, self.wt_out_even, self.wt_in_odd, self.wt_out_odd] + self.wt_mlp_in + self.wt_mlp_out)


    def _b1_gen(self):
        k, L = self.k, self.L
        if True:
            up = [k.sb([128, 4, 528], F32, "up") for _ in range(2)]
            ta = [k.sb([128, 528], F32, "pta") for _ in range(4)]
            tb = [k.sb([128, 528], F32, "ptb") for _ in range(4)]
            dd = [k.sb([128, 4, 512], BF16, "pd") for _ in range(2)]
            yb = [k.sb([128, 4, 512], BF16, "pyb") for _ in range(2)]
            pwf = k.sb([128, 4, 128], F32, "pwf")
            pwb = k.sb([128, 4, 128], BF16, "pwb")
            psc = k.sb([128, 4], F32, "psc")
            psY = [k.ps([128, 512], F32, "psY") for _ in range(2)]
            k.dma("sp", pwf[:], self.pool_w.t.rearrange("g c d -> c g d"), self.pool_w, pwf)
            k.dma("sp", psc[:], self.pool_scale[:, :], self.pool_scale, psc)
            k.op("dve", cp(pwb[:], pwf[:]), [pwf], [pwb])
            tilesB = [(t0, min(512, L - t0)) for t0 in range(0, L, 512)]
            for ti, (t0, w) in enumerate(tilesB):
                U, Dd, Y = up[ti % 2], dd[ti % 2], yb[ti % 2]
                lo, hi = t0 - 8, t0 + w + 8
                clo, chi = max(lo, 0), min(hi, L)
                if clo != lo or chi != hi:
                    k.op("dve", mset(U[:], 0.0), [], [U])
                k.dma("sp", U[:, :, clo - lo:chi - lo], self.UP.t[:, :, clo:chi].rearrange("g p t -> p g t"), self.UP, U)
                n = w + 16
                for g in range(4):
                    W = 2 << g
                    eng = "dve" if g < 2 else "pool"
                    A, B = ta[g], tb[g]
                    k.op(eng, tt(A[:, 1:n], U[:, g, 0:n - 1], U[:, g, 1:n], ALU.add), [U], [A])
                    cur, oth, lo_b, hi_b, sh = A, B, 1, n, 1
                    for lvl in range(g):
                        nlo, nhi = lo_b + sh, hi_b - sh
                        k.op(eng, tt(oth[:, nlo:nhi], cur[:, nlo - sh:nhi - sh], cur[:, nlo + sh:nhi + sh], ALU.add), [cur], [oth])
                        cur, oth, lo_b, hi_b, sh = oth, cur, nlo, nhi, sh * 2
                    assert lo_b <= 8 and hi_b >= w + 8
                    k.op("dve", stt(Dd[:, g, 0:w], cur[:, 8:8 + w], 1.0 / W, U[:, g, 8:8 + w], ALU.mult, ALU.subtract), [cur, U], [Dd])
                    fix = []
                    for t in range(t0, t0 + w):
                        lo_t, hi_t = max(t - W // 2, 0), min(t + (W - 1 - W // 2), L - 1)
                        cnt = hi_t - lo_t + 1
                        if cnt != W:
                            fix.append((t, cnt))
                    for (t, cnt) in fix:
                        b = t - t0
                        k.op("dve", stt(Dd[:, g, b:b + 1], cur[:, 8 + b:9 + b], 1.0 / cnt, U[:, g, 8 + b:9 + b], ALU.mult, ALU.subtract), [cur, U], [Dd])
                    py = psY[g % 2]
                    k.op("pe", mm(py[:, 0:w], pwb[:, g, :], Dd[:, g, 0:w]), [pwb, Dd], [py])
                    k.op("act", act(Y[:, g, 0:w], py[:, 0:w], AF.Copy, scale=psc[:, g:g + 1]), [py, psc], [Y])
                k.dma("pool", self.YM.t[0:4, :, t0:t0 + w].rearrange("g p t -> p g t"), Y[:, :, 0:w], Y, self.YM)
                yield

    def _b2_gen(self):
        k, L = self.k, self.L
        if True:
            uq = [k.sb([128, 12, 518], BF16, "uq") for _ in range(2)]
            cwf = k.sb([128, 12, 7], F32, "cwf")
            dg = k.sb([128, 84, 128], BF16, "dg")
            cs = k.sb([128, 12, 512], F32, "cs")
            sq = [k.sb([128, 512], BF16, "sq2") for _ in range(2)]
            tmp = k.sb([128, 8, 512], F32, "tmp2")
            rs = k.sb([128, 8, 512], F32, "rs2")
            ob = [k.sb([128, 12, 512], BF16, "ob2") for _ in range(2)]
            psC = [k.ps([128, 512], F32, "psC") for _ in range(3)]
            psN = [k.ps([128, 512], F32, "psN") for _ in range(2)]
            k.dma("sp", cwf[:], self.conv_w[:, :, :], self.conv_w, cwf)
            for ch in range(12):
                for j in range(7):
                    k.op("dve" if (ch + j) % 2 else "pool", tsc(dg[:, ch * 7 + j, :], self.identb[:], cwf[:, ch, j:j + 1], ALU.mult), [self.identb, cwf], [dg])
            for ti, (t0, w) in enumerate(self.tiles):
                U, O = uq[ti % 2], ob[ti % 2]
                lo, hi = t0 - 3, t0 + w + 3
                clo, chi = max(lo, 0), min(hi, L)
                if clo != lo or chi != hi:
                    k.op("pool", mset(U[:], 0.0), [], [U])
                k.dma("sp", U[:, :, clo - lo:chi - lo], self.UQ.t[:, :, clo:chi].rearrange("g p t -> p g t"), self.UQ, U)
                for ch in range(12):
                    pc = psC[ch % 3]
                    k.op("pe", [mm(pc[:, 0:w], dg[:, ch * 7 + j, :], U[:, ch, j:j + w], j == 0, j == 6) for j in range(7)], [dg, U], [pc])
                    k.op("act", act(cs[:, ch, 0:w], pc[:, 0:w], AF.Silu), [pc], [cs])
                for ch in range(8):
                    s = sq[ch % 2]
                    pn = psN[ch % 2]
                    k.op("dve", tt(s[:, 0:w], cs[:, ch, 0:w], cs[:, ch, 0:w], ALU.mult), [cs], [s])
                    k.op("pe", mm(pn[:, 0:w], self.onesb[:], s[:, 0:w]), [s, self.onesb], [pn])
                    if ch < 4:
                        k.op("act", act(tmp[:, ch, 0:w], pn[:, 0:w], AF.Ln, bias=self.eps128[:, 0:1], scale=128.0), [pn, self.eps128], [tmp])
                    else:
                        k.op("act", act(tmp[:, ch, 0:w], pn[:, 0:w], AF.Ln, bias=self.epsc[:, 0:1], scale=1.0), [pn, self.epsc], [tmp])
                k.op("act", act(rs[:, :, 0:w], tmp[:, :, 0:w], AF.Exp, scale=-0.5), [tmp], [rs])
                for ch in range(8):
                    k.op("pool" if ch % 2 else "dve", tt(O[:, ch, 0:w], cs[:, ch, 0:w], rs[:, ch, 0:w], ALU.mult), [cs, rs], [O])
                k.op("pool", cp(O[:, 8:12, 0:w], cs[:, 8:12, 0:w]), [cs], [O])
                k.dma("pool", self.QKVN.t[:, :, t0:t0 + w].rearrange("g p t -> p g t"), O[:, :, 0:w], O, self.QKVN)
                yield

    def phase_b12(self):
        k = self.k
        k.push_phase()
        with k.phase:
            gens = [self._b1_gen(), self._b2_gen()]
            live = [True, True]
            while any(live):
                for i in range(2):
                    if live[i]:
                        try:
                            next(gens[i])
                        except StopIteration:
                            live[i] = False
            k.barrier([self.YM, self.QKVN])

    def phase_b3(self):
        k, L = self.k, self.L
        k.push_phase()
        with k.phase:
            NCH = len(self.chunks)
            msk = k.sb([128, 6, 128], F32, "msk")
            k.dma("sp", msk[:], self.c_masks[:, :, :], self.c_masks, msk)
            mskb = k.sb([128, 6, 128], BF16, "mskb")
            k.op("dve", cp(mskb[:], msk[:]), [msk], [mskb])
            I4 = k.sb([128, 4, 128], F32, "I4")
            negmask4 = [k.sb([128, 4, 128], F32, "negm4") for _ in range(2)]
            Um4 = [k.sb([128, 4, 128], F32, "Um4") for _ in range(2)]
            for h in range(4):
                k.op("pool", cp(I4[:, h, :], self.ident[:]), [self.ident], [I4])
                for d in range(2):
                    k.op("pool", cp(negmask4[d][:, h, :], msk[:, d, :]), [msk], [negmask4[d]])
                    k.op("pool", cp(Um4[d][:, h, :], msk[:, 2 + d, :]), [msk], [Um4[d]])
            alog = k.sb([128, 8], F32, "alog")
            dtb = k.sb([128, 8], F32, "dtb")
            nea = k.sb([128, 8], F32, "nea")
            k.dma("sp", alog[:], self.a_log[:, :], self.a_log, alog)
            k.dma("sp", dtb[:], self.dt_bias[:, :], self.dt_bias, dtb)
            k.op("act", act(nea[:], alog[:], AF.Exp), [alog], [nea])
            k.op("dve", tsc(nea[:], nea[:], -1.0, ALU.mult), [nea], [nea])
            BA = k.sb([128, NCH, 16], F32, "gBA")
            k.op("pool", mset(BA[:, 0, :], 0.0), [], [BA])
            k.dma("sp", BA[0:NMETA, 0, :], self.UBA.t[0:NMETA, :], self.UBA, BA)
            k.dma("sp", BA[:, 1:NCH, :], self.UBA.t[NMETA:L, :].rearrange("(c p) f -> p c f", p=128), self.UBA, BA)
            sm = {n_: k.sb([128, NCH, 8], F32, "g" + n_) for n_ in ("beta", "nbeta", "sp", "g", "gc", "gl", "egc", "kds", "egl", "bw")}
            ghl = k.sb([128, 2, NCH, 8], BF16, "ghl")
            ghf = k.sb([128, 2, NCH, 8], F32, "ghf")
            bc8 = lambda ap: ap.unsqueeze(1).broadcast_to([128, NCH, 8])
            k.op("act", act(sm["beta"][:], BA[:, :, 0:8], AF.Exp, scale=-1.0), [BA], [sm["beta"]])
            k.op("dve", tsc(sm["beta"][:], sm["beta"][:], 1.0, ALU.add), [sm["beta"]], [sm["beta"]])
            k.op("dve", rcp(sm["beta"][:], sm["beta"][:]), [sm["beta"]], [sm["beta"]])
            k.op("dve", tt(sm["sp"][:], BA[:, :, 8:16], bc8(dtb[:]), ALU.add), [BA, dtb], [sm["sp"]])
            k.op("act", act(sm["sp"][:], sm["sp"][:], AF.Exp), [sm["sp"]], [sm["sp"]])
            k.op("act", act(sm["sp"][:], sm["sp"][:], AF.Ln, bias=self.onec[:, 0:1]), [sm["sp"], self.onec], [sm["sp"]])
            k.op("dve", tt(sm["g"][:], sm["sp"][:], bc8(nea[:]), ALU.mult), [sm["sp"], nea], [sm["g"]])
            k.op("dve", tsc(sm["beta"][:, 0, :], sm["beta"][:, 0, :], msk[:, 4, 0:1], ALU.mult), [sm["beta"], msk], [sm["beta"]])
            k.op("dve", tsc(sm["g"][:, 0, :], sm["g"][:, 0, :], msk[:, 4, 0:1], ALU.mult), [sm["g"], msk], [sm["g"]])
            k.op("dve", tsc(sm["nbeta"][:], sm["beta"][:], -1.0, ALU.mult), [sm["beta"]], [sm["nbeta"]])
            k.op("dve", cp(ghl[:, 0], sm["g"][:]), [sm["g"]], [ghl])
            k.op("dve", cp(ghf[:, 0], ghl[:, 0]), [ghl], [ghf])
            k.op("dve", tt(ghf[:, 1], sm["g"][:], ghf[:, 0], ALU.subtract), [sm["g"], ghf], [ghf])
            k.op("dve", cp(ghl[:, 1], ghf[:, 1]), [ghf], [ghl])
            k.op("dve", cp(ghf[:, 1], ghl[:, 1]), [ghl], [ghf])
            pA = [k.ps([128, 4, 128], F32, "pA") for _ in range(2)]
            pB = [k.ps([128, 4, 128], F32, "pB") for _ in range(2)]
            pC = [k.ps([128, 4, 128], F32, "pC") for _ in range(2)]
            pT = [k.ps([128, 2, 4, 128], BF16, "pT") for _ in range(2)]
            NS = NCH * 4
            for d in range(2):
                dsl = slice(d * 4, d * 4 + 4)
                fa = pA[d].t[:].rearrange("p h j -> p (h j)")
                fb = pB[d].t[:].rearrange("p h j -> p (h j)")
                o3 = lambda f: f[:, 0:NS].rearrange("p (c h) -> p c h", h=4)
                k.op("pe", [mm(o3(fa), mskb[:, 2 + d, :], ghl[:, 0, :, dsl], True, False), mm(o3(fa), mskb[:, 2 + d, :], ghl[:, 1, :, dsl], False, True)],
                     [mskb, ghl], [pA[d]])
                k.op("pe", [mm(o3(fb), self.onesb[:], ghl[:, 0, :, dsl], True, False), mm(o3(fb), self.onesb[:], ghl[:, 1, :, dsl], False, True)],
                     [self.onesb, ghl], [pB[d]])
                k.op("dve", cp(sm["gc"][:, :, dsl], o3(fa)), [pA[d]], [sm["gc"]])
                k.op("dve", cp(sm["gl"][:, :, dsl], o3(fb)), [pB[d]], [sm["gl"]])
            k.op("act", act(sm["egc"][:], sm["gc"][:], AF.Exp), [sm["gc"]], [sm["egc"]])
            k.op("act", act(sm["egl"][:], sm["gl"][:], AF.Exp), [sm["gl"]], [sm["egl"]])
            k.op("dve", tt(sm["kds"][:], sm["gl"][:], sm["gc"][:], ALU.subtract), [sm["gl"], sm["gc"]], [sm["kds"]])
            k.op("act", act(sm["kds"][:], sm["kds"][:], AF.Exp), [sm["kds"]], [sm["kds"]])
            k.op("dve", tt(sm["bw"][:], sm["beta"][:], sm["egc"][:], ALU.mult), [sm["beta"], sm["egc"]], [sm["bw"]])

            def make_dir(d):
                dsl = slice(d * 4, d * 4 + 4)
                NB = 2
                dbl = lambda shape, dt, name: [k.sb(shape, dt, name + str(d)) for _ in range(NB)]
                qkv = dbl([128, 12, 128], BF16, "gqkv")
                Bm = dbl([128, 2, 4, 128], BF16, "gBm")
                arg = dbl([128, 4, 128], F32, "garg")
                Ds = dbl([128, 4, 128], F32, "gDs")
                Dsn = dbl([128, 4, 128], F32, "gDsn")
                Di = dbl([128, 4, 128], F32, "gDi")
                EGB = dbl([128, 4, 128], F32, "gEGB")
                qdT = dbl([128, 4, 128], BF16, "gqdT")
                Xa = dbl([128, 4, 128], BF16, "gXa")
                Xb = dbl([128, 4, 128], BF16, "gXb")
                XI = dbl([128, 4, 128], BF16, "gXI")
                Yb = dbl([128, 4, 128], BF16, "gYb")
                att = dbl([128, 4, 128], BF16, "gatt")
                YA = dbl([128, 2, 4, 128], BF16, "gYA")
                Qa = dbl([128, 4, 128], BF16, "gQa")
                Qc = dbl([128, 4, 128], BF16, "gQc")
                kvtok = dbl([128, 2, 4, 128], BF16, "gkvtok")
                vb = dbl([128, 4, 128], BF16, "gvb")
                kbg = dbl([128, 4, 128], BF16, "gkbg")
                kdec = dbl([128, 4, 128], BF16, "gkdec")
                nwcT = dbl([128, 4, 128], BF16, "gnwcT")
                vn = dbl([128, 4, 128], BF16, "gvn")
                osb = dbl([128, 4, 128], F32, "gosb")
                Sf = k.sb([128, 4, 128], F32, "gSf" + str(d))
                Sb = k.sb([128, 4, 128], BF16, "gSb" + str(d))
                k.op("dve", mset(Sf[:], 0.0), [], [Sf])
                k.op("pool", mset(Sb[:], 0.0), [], [Sb])
                A_, B_, C_, T_ = pA[d], pB[d], pC[d], pT[d]
                Qfin = {}
                bc = lambda ap: ap.unsqueeze(2).broadcast_to([128, 4, 128])

                def prep(ci, n):
                    t0, w = self.chunks[ci]
                    b = n % NB
                    Q_ = qkv[b]
                    if w < 128:
                        k.op("pool", mset(Q_[:], 0.0), [], [Q_])
                    k.dma("sp", Q_[:, :, 0:w], self.QKVN.t[:, :, t0:t0 + w].rearrange("g p t -> p g t"), self.QKVN, Q_)
                    for hl in range(2):
                        k.op("pool", tt(Bm[b][:, hl], Um4[d][:], bc(ghf[:, hl, ci, dsl]), ALU.mult), [Um4[d], ghf], [Bm[b]])
                    k.op("pe", [f for h in range(4) for f in (mm(C_[:, h, :], self.onesb[:], Bm[b][:, 0, h, :], True, False),
                                                              mm(C_[:, h, :], self.onesb[:], Bm[b][:, 1, h, :], False, True))], [self.onesb, Bm[b]], [C_])
                    k.op("pe", [mm(A_[:, h, :], Q_[:, 4 + h, :], Q_[:, 4 + h, :]) for h in range(4)], [Q_], [A_])
                    k.op("pe", [mm(B_[:, h, :], Q_[:, h, :], Q_[:, 4 + h, :]) for h in range(4)], [Q_], [B_])
                    k.op("pe", [tr(T_[:, 0, h, :], Q_[:, 4 + h, :], self.identb[:]) for h in range(4)] + [tr(T_[:, 1, h, :], Q_[:, 8 + h, :], self.identb[:]) for h in range(4)],
                         [Q_, self.identb], [T_])
                    k.op("dve", stt(arg[b][:], C_[:], -1.0, negmask4[d][:], ALU.mult, ALU.add), [C_, negmask4[d]], [arg[b]])
                    k.op("act", act(EGB[b][:], C_[:], AF.Exp), [C_], [EGB[b]])
                    k.op("dve", tt(arg[b][:], arg[b][:], bc(sm["gc"][:, ci, dsl]), ALU.add), [arg[b], sm["gc"]], [arg[b]])
                    k.op("act", act(Ds[b][:], arg[b][:], AF.Exp), [arg[b]], [Ds[b]])
                    k.op("dve", cp(kvtok[b][:], T_[:]), [T_], [kvtok[b]])
                    yield
                    k.op("pool", tt(Dsn[b][:], Ds[b][:], bc(sm["nbeta"][:, ci, dsl]), ALU.mult), [Ds[b], sm["nbeta"]], [Dsn[b]])
                    k.op("pool", tt(Di[b][:], Ds[b][:], I4[:], ALU.add), [Ds[b], I4], [Di[b]])
                    k.op("dve", tt(Xa[b][:], A_[:], Dsn[b][:], ALU.mult), [A_, Dsn[b]], [Xa[b]])
                    k.op("dve", tt(att[b][:], B_[:], Di[b][:], ALU.mult), [B_, Di[b]], [att[b]])
                    k.op("pe", [tr(T_[:, 0, h, :], Xa[b][:, h, :], self.identb[:]) for h in range(4)] + [tr(T_[:, 1, h, :], att[b][:, h, :], self.identb[:]) for h in range(4)],
                         [Xa[b], att[b], self.identb], [T_])
                    k.op("act", acp(YA[b][:], T_[:]), [T_], [YA[b]])
                    k.op("dve", tt(Qa[b][:], I4[:], YA[b][:, 0], ALU.add), [I4, YA[b]], [Qa[b]])
                    k.op("pool", tt(qdT[b][:], Q_[:, 0:4, :], EGB[b][:], ALU.mult), [Q_, EGB[b]], [qdT[b]])
                    k.op("pool", tt(vb[b][:], kvtok[b][:, 1], bc(sm["beta"][:, ci, dsl]), ALU.mult), [kvtok[b], sm["beta"]], [vb[b]])
                    k.op("pool", tt(kbg[b][:], kvtok[b][:, 0], bc(sm["bw"][:, ci, dsl]), ALU.mult), [kvtok[b], sm["bw"]], [kbg[b]])
                    k.op("pool", tt(kdec[b][:], kvtok[b][:, 0], bc(sm["kds"][:, ci, dsl]), ALU.mult), [kvtok[b], sm["kds"]], [kdec[b]])
                    yield
                    Xc, Xn = Xa[b], Xb[b]
                    ybuf, yap = YA[b], (lambda h, Y_=YA[b]: Y_[:, 0, h, :])
                    ynext = [Yb[b], XYalt[b]]
                    Qcur, Qnxt = Qa[b], Qc[b]
                    for lvl in range(6):
                        k.op("pe", [mm(A_[:, h, :], yap(h), Xc[:, h, :]) for h in range(4)], [Xc, ybuf], [A_])
                        if lvl < 5:
                            k.op("pe", [mm(B_[:, h, :], Xc[:, h, :], yap(h)) for h in range(4)], [Xc, ybuf], [B_])
                        k.op("act", acp(Xn[:], A_[:]), [A_], [Xn])
                        k.op("dve", tt(XI[b][:], A_[:], I4[:], ALU.add), [A_, I4], [XI[b]])
                        if lvl < 5:
                            Yn = ynext[lvl % 2]
                            k.op("dve", cp(Yn[:], B_[:]), [B_], [Yn])
                        yield
                        k.op("pe", [mm(C_[:, h, :], XI[b][:, h, :], Qcur[:, h, :]) for h in range(4)], [XI[b], Qcur], [C_])
                        k.op("act", acp(Qnxt[:], C_[:]), [C_], [Qnxt])
                        Qcur, Qnxt = Qnxt, Qcur
                        Xc, Xn = Xn, Xc
                        if lvl < 5:
                            ybuf, yap = Yn, (lambda h, Y_=Yn: Y_[:, h, :])
                        yield
                    Qfin[b] = Qcur
                    k.op("pe", [mm(A_[:, h, :], kbg[b][:, h, :], Qcur[:, h, :]) for h in range(4)], [kbg[b], Qcur], [A_])
                    k.op("act", act(nwcT[b][:], A_[:], AF.Copy, scale=-1.0), [A_], [nwcT[b]])
                    yield

                XYalt = dbl([128, 4, 128], BF16, "gYalt")

                def scan(ci, n):
                    t0, w = self.chunks[ci]
                    b = n % NB
                    Qb = Qfin[b]
                    k.op("pe", [f for h in range(4) for f in (mm(C_[:, h, :], Qb[:, h, :], vb[b][:, h, :], True, False),
                                                              mm(C_[:, h, :], nwcT[b][:, h, :], Sb[:, h, :], False, True))],
                         [Qb, vb[b], nwcT[b], Sb], [C_])
                    k.op("act", acp(vn[b][:], C_[:]), [C_], [vn[b]])
                    yield
                    k.op("pe", [f for h in range(4) for f in (mm(A_[:, h, :], Sb[:, h, :], qdT[b][:, h, :], True, False),
                                                              mm(A_[:, h, :], vn[b][:, h, :], YA[b][:, 1, h, :], False, True))],
                         [Sb, qdT[b], vn[b], YA[b]], [A_])
                    k.op("pe", [mm(B_[:, h, :], kdec[b][:, h, :], vn[b][:, h, :]) for h in range(4)], [kdec[b], vn[b]], [B_])
                    for h in range(4):
                        k.op("dve", stt(Sf[:, h, :], Sf[:, h, :], sm["egl"][:, ci, d * 4 + h:d * 4 + h + 1], B_[:, h, :], ALU.mult, ALU.add), [Sf, sm["egl"], B_], [Sf])
                    k.op("act", acp(Sb[:], Sf[:]), [Sf], [Sb])
                    k.op("act", acp(osb[b][:], A_[:]), [A_], [osb[b]])
                    k.dma("pool", self.OF.t[d, :, :, t0:t0 + w].rearrange("h p t -> p h t"), osb[b][:, :, 0:w], osb[b], self.OF)
                    yield

                def run():
                    order = list(range(NCH))
                    if d == 1:
                        order = order[::-1]
                    yield from prep(order[0], 0)
                    for n, ci in enumerate(order):
                        if n + 1 < len(order):
                            yield from prep(order[n + 1], n + 1)
                        yield from scan(ci, n)
                return run()

            gens = [make_dir(0), make_dir(1)]
            live = [True, True]
            while any(live):
                for i in range(2):
                    if live[i]:
                        try:
                            next(gens[i])
                        except StopIteration:
                            live[i] = False
            k.barrier([self.OF])

    def phase_b4(self):
        k, L = self.k, self.L
        k.push_phase()
        with k.phase:
            of = [k.sb([128, 2, 4, 512], F32, "b4of") for _ in range(2)]
            uz = [k.sb([128, 4, 512], F32, "b4uz") for _ in range(2)]
            o = k.sb([128, 4, 512], F32, "b4o")
            sq = [k.sb([128, 512], BF16, "b4sq") for _ in range(2)]
            tmp = k.sb([128, 4, 512], F32, "b4tmp")
            rs = k.sb([128, 4, 512], F32, "b4rs")
            zs = k.sb([128, 4, 512], F32, "b4zs")
            yb = [k.sb([128, 4, 512], BF16, "b4y") for _ in range(2)]
            dn = k.sb([128, 1], F32, "b4dn")
            psN = [k.ps([128, 512], F32, "b4ps") for _ in range(2)]
            k.dma("sp", dn[:], self.delta_norm[:, :], self.delta_norm, dn)
            tilesB = [(t0, min(512, L - t0)) for t0 in range(0, L, 512)]
            for ti, (t0, w) in enumerate(tilesB):
                OFt, UZt, Y = of[ti % 2], uz[ti % 2], yb[ti % 2]
                for dd_ in range(2):
                    k.dma("sp", OFt[:, dd_, :, 0:w], self.OF.t[dd_, :, :, t0:t0 + w].rearrange("h p t -> p h t"), self.OF, OFt)
                k.dma("sp", UZt[:, :, 0:w], self.UZ.t[:, :, t0:t0 + w].rearrange("h p t -> p h t"), self.UZ, UZt)
                k.op("pool", tt(o[:, :, 0:w], OFt[:, 0, :, 0:w], OFt[:, 1, :, 0:w], ALU.add), [OFt], [o])
                k.op("act", act(zs[:, :, 0:w], UZt[:, :, 0:w], AF.Silu), [UZt], [zs])
                for h in range(4):
                    s_, pn = sq[h % 2], psN[h % 2]
                    k.op("act", act(s_[:, 0:w], o[:, h, 0:w], AF.Square), [o], [s_])
                    k.op("pe", mm(pn[:, 0:w], self.onesb[:], s_[:, 0:w]), [s_, self.onesb], [pn])
                    k.op("act", act(tmp[:, h, 0:w], pn[:, 0:w], AF.Ln, bias=self.epsc[:, 0:1], scale=1.0 / 128), [pn, self.epsc], [tmp])
                k.op("act", act(rs[:, :, 0:w], tmp[:, :, 0:w], AF.Exp, scale=-0.5), [tmp], [rs])
                k.op("pool", tt(o[:, :, 0:w], o[:, :, 0:w], rs[:, :, 0:w], ALU.mult), [o, rs], [o])
                for h in range(4):
                    k.op("dve", stt(Y[:, h, 0:w], o[:, h, 0:w], dn[:, 0:1], zs[:, h, 0:w], ALU.mult, ALU.mult), [o, dn, zs], [Y])
                k.dma("pool", self.YM.t[4:8, :, t0:t0 + w].rearrange("g p t -> p g t"), Y[:, :, 0:w], Y, self.YM)
            k.barrier([self.YM])


    def phase_ce(self, slot, layer):
        k, L = self.k, self.L
        k.push_phase()
        with k.phase:
            H = [k.sb([128, 8, 512], F32, "ceH") for _ in range(2)]
            M = k.sb([128, 8, 512], BF16, "ceM")
            sq = [k.sb([128, 512], BF16, "cesq") for _ in range(2)]
            tmp = k.sb([128, 512], F32, "cetmp")
            rstd = k.sb([128, 512], F32, "cerstd")
            xn1 = [k.sb([128, 8, 512], BF16, "cexn") for _ in range(2)]
            h1 = [k.sb([128, 8, 512], BF16, "ceh1") for _ in range(4)]
            r = [k.sb([128, 512], F32, "cer") for _ in range(2)]
            wo = [k.sb([128, 1024], BF16, "cewo") for _ in range(4)]
            w1 = [k.sb([128, 2, 1024], BF16, "cew1") for _ in range(4)]
            w2 = [k.sb([128, 4096], BF16, "cew2") for _ in range(2)]
            psM = [k.ps([128, 512], F32, "cepsM") for _ in range(2)]
            psS = k.ps([128, 512], F32, "cepsS")
            psF = [k.ps([128, 512], F32, "cepsF") for _ in range(2 if layer == 0 else 3)]
            NF = len(psF)
            if layer == 0:
                psQ = [k.ps([128, 512], F32, "cepsQ") for _ in range(2)]
                xn2 = k.sb([128, 8, 512], BF16, "cexn2")
                wq = [k.sb([128, 3, 1024], BF16, "cewq") for _ in range(2)]
                rope = k.sb([128, 2, 512], F32, "cerope")
                qkn = k.sb([128, 2], F32, "ceqkn")
                permf = k.sb([128, 128], F32, "cepermf")
                permb = k.sb([128, 128], BF16, "cepermb")
                sqh = [k.sb([128, 512], BF16, "cesqh") for _ in range(2)]
                tmpq = [k.sb([128, 512], F32, "cetmpq") for _ in range(2)]
                qg = [k.sb([128, 512], F32, "ceqg") for _ in range(2)]
                qgb = [k.sb([128, 512], BF16, "ceqgb") for _ in range(2)]
                t1 = k.sb([128, 512], F32, "cet1")
                t2 = k.sb([128, 512], F32, "cet2")
                obq = k.sb([128, 8, 512], BF16, "ceobq")
                obk = k.sb([128, 2, 512], BF16, "ceobk")
                vb = k.sb([128, 4, 256], BF16, "cevb")
                psV = k.ps([128, 2, 256], F32, "cepsV")
                k.dma("sp", qkn[:], self.qk_norm[:, :], self.qk_norm, qkn)
                k.dma("sp", permf[:], self.c_perm[:, :], self.c_perm, permf)
                k.op("dve", cp(permb[:], permf[:]), [permf], [permb])
                Hsrc, Msrc, wt_out = self.H0, self.YM, self.wt_out_even
            else:
                yt = k.sb([128, 4, D], F32, "ceyt")
                Hsrc, Msrc, wt_out = self.H1, self.AT, self.wt_out_odd
            tiles = [t for ti, t in enumerate(self.tiles) if not (layer == 1 and ti == 0)]
            cnt = {"wo": 0, "w1": 0, "w2": 0, "wq": 0}

            def S1(i):
                t0, w = tiles[i]
                Hb, xn = H[i % 2], xn1[i % 2]
                k.dma("sp", Hb[:, :, 0:w], Hsrc.t[:, :, t0:t0 + w].rearrange("c p t -> p c t"), Hsrc, Hb)
                k.dma("sp", M[:, :, 0:w], Msrc.t[:, :, t0:t0 + w].rearrange("c p t -> p c t"), Msrc, M)
                for m in range(8):
                    W = wo[cnt["wo"] % 4]; cnt["wo"] += 1
                    k.dma("sp", W[:, :], wt_out.t[m, :, :], wt_out, W)
                    pm = psM[m % 2]
                    k.op("pe", [mm(pm[:, 0:w], W[:, c * 128:(c + 1) * 128], M[:, c, 0:w], c == 0, c == 7) for c in range(8)], [W, M], [pm])
                    k.op("dve", tt(Hb[:, m, 0:w], Hb[:, m, 0:w], pm[:, 0:w], ALU.add), [Hb, pm], [Hb])
                self.rms_to_bf16(Hb, w, sq, psS, tmp, rstd, xn)

            def S2(i):
                t0, w = tiles[i]
                xn = xn1[i % 2]
                for fp in range(16):
                    W = w1[cnt["w1"] % 4]; cnt["w1"] += 1
                    k.dma("sp", W[:, :, :], self.wt_mlp_in[layer].t[2 * fp:2 * fp + 2, :, :].rearrange("m p c -> p m c"), self.wt_mlp_in[layer], W)
                    for fi in range(2):
                        f = 2 * fp + fi
                        pf, rr, hg = psF[f % NF], r[f % 2], h1[f // 8]
                        k.op("pe", [mm(pf[:, 0:w], W[:, fi, c * 128:(c + 1) * 128], xn[:, c, 0:w], c == 0, c == 7) for c in range(8)], [W, xn], [pf])
                        k.op("act", act(rr[:, 0:w], pf[:, 0:w], AF.Relu), [pf], [rr])
                        k.op("pool", tt(hg[:, f % 8, 0:w], rr[:, 0:w], rr[:, 0:w], ALU.mult), [rr], [hg])
                    yield

            def S3a(i):
                t0, w = tiles[i]
                Hb = H[i % 2]
                for m in range(8):
                    W = w2[cnt["w2"] % 2]; cnt["w2"] += 1
                    k.dma("sp", W[:, :], self.wt_mlp_out[layer].t[m, :, :], self.wt_mlp_out[layer], W)
                    pm = psM[m % 2]
                    for fg in range(4):
                        k.op("pe", [mm(pm[:, 0:w], W[:, f * 128:(f + 1) * 128], h1[fg][:, f % 8, 0:w], f == 0, f == 31) for f in range(fg * 8, fg * 8 + 8)],
                             [W, h1[fg]], [pm])
                    k.op("dve", tt(Hb[:, m, 0:w], Hb[:, m, 0:w], pm[:, 0:w], ALU.add), [Hb, pm], [Hb])
                if layer == 0:
                    k.dma("pool", self.H1.t[:, :, t0:t0 + w].rearrange("c p t -> p c t"), Hb[:, :, 0:w], Hb, self.H1)
                    self.rms_to_bf16(Hb, w, sq, psS, tmp, rstd, xn2)
                else:
                    for j in range(4):
                        for cg in range(2):
                            pf = psF[(2 * j + cg) % 3]
                            k.op("pe", [tr(pf[:, ci * 128:(ci + 1) * 128], Hb[:, cg * 4 + ci, j * 128:(j + 1) * 128], self.ident[:]) for ci in range(4)], [Hb, self.ident], [pf])
                            k.op("act" if cg else "dve", (acp if cg else cp)(yt[:, j, cg * 512:(cg + 1) * 512], pf[:, :]), [pf], [yt])
                    r0 = t0 - NMETA
                    k.dma("pool", self.y[slot].t[r0:r0 + 512, :].rearrange("(j p) d -> p j d", p=128), yt[:, :, :], yt, self.y[slot])

            def S3b(i):
                t0, w = tiles[i]
                nsub = (w + 127) // 128
                pw = min(w, 128)
                xn = xn2
                k.dma("sp", rope[:, :, 0:w], self.c_rope.t[:, :, t0:t0 + w].rearrange("a p t -> p a t"), self.c_rope, rope)
                Wof = {}

                def wload(g):
                    W = wq[cnt["wq"] % 2]; cnt["wq"] += 1
                    k.dma("sp", W[:, :, :], self.wt_in_odd.t[3 * g:3 * g + 3, :, :].rearrange("m p c -> p m c"), self.wt_in_odd, W)
                    Wof[g] = W

                def P(m):
                    g, mi = divmod(m, 3)
                    if mi == 0:
                        wload(g)
                    W, pf, i2 = Wof[g], psQ[m % 2], m % 2
                    k.op("pe", [mm(pf[:, 0:w], W[:, mi, c * 128:(c + 1) * 128], xn[:, c, 0:w], c == 0, c == 7) for c in range(8)], [W, xn], [pf])
                    k.op("act", act(sqh[i2][:, 0:w], pf[:, 0:w], AF.Square), [pf], [sqh[i2]])

                def N(m):
                    pf, i2 = psQ[m % 2], m % 2
                    k.op("pe", mm(psS[:, 0:w], self.onesb[:], sqh[i2][:, 0:w]), [sqh[i2], self.onesb], [psS])
                    k.op("act", act(tmpq[i2][:, 0:w], psS[:, 0:w], AF.Ln, bias=self.epsc[:, 0:1], scale=1.0 / 128), [psS, self.epsc], [tmpq[i2]])
                    k.op("act", act(tmpq[i2][:, 0:w], tmpq[i2][:, 0:w], AF.Exp, scale=-0.5), [tmpq[i2]], [tmpq[i2]])
                    gc_ = qkn[:, 0:1] if m < 8 else qkn[:, 1:2]
                    k.op("dve", stt(qg[i2][:, 0:w], pf[:, 0:w], gc_, tmpq[i2][:, 0:w], ALU.mult, ALU.mult), [pf, qkn, tmpq[i2]], [qg[i2]])
                    k.op("pool", cp(qgb[i2][:, 0:w], qg[i2][:, 0:w]), [qg[i2]], [qgb[i2]])

                def R(m):
                    i2 = m % 2
                    pm = psM[m % 2]
                    k.op("pe", mm(pm[:, 0:w], permb[:], qgb[i2][:, 0:w]), [permb, qgb[i2]], [pm])
                    k.op("pool", tt(t1[:, 0:w], qg[i2][:, 0:w], rope[:, 0, 0:w], ALU.mult), [qg[i2], rope], [t1])
                    k.op("dve", tt(t2[:, 0:w], pm[:, 0:w], rope[:, 1, 0:w], ALU.mult), [pm, rope], [t2])
                    if m < 8:
                        k.op("pool", tt(obq[:, m, 0:w], t1[:, 0:w], t2[:, 0:w], ALU.add), [t1, t2], [obq])
                        if m == 7:
                            k.dma("pool", self.QT.t[:, :, t0:t0 + w].rearrange("g p t -> p g t"), obq[:, :, 0:w], obq, self.QT)
                    else:
                        k.op("pool", tt(obk[:, m - 8, 0:w], t1[:, 0:w], t2[:, 0:w], ALU.add), [t1, t2], [obk])
                        if m == 9:
                            k.dma("pool", self.KT.t[:, :, t0:t0 + w].rearrange("g p t -> p g t"), obk[:, :, 0:w], obk, self.KT)

                for m in range(10):
                    P(m)
                    yield
                    N(m)
                    yield
                    if m >= 1:
                        R(m - 1)
                        yield
                R(9)
                yield
                W = Wof[3]
                for jp in range(0, nsub, 2):
                    js = list(range(jp, min(jp + 2, nsub)))
                    k.op("pe", [mm(psV[0:pw, j % 2, half * 128:(half + 1) * 128], xn[:, c, j * 128:j * 128 + pw], W[:, 1 + half, c * 128:(c + 1) * 128], c == 0, c == 7)
                                for j in js for half in range(2) for c in range(8)], [W, xn], [psV])
                    k.op("act", acp(vb[0:pw, jp:jp + len(js), :], psV[0:pw, 0:len(js), :]), [psV], [vb])
                    yield
                k.dma("pool", self.VT.t[t0:t0 + w, :].rearrange("(j p) f -> p j f", p=pw), vb[0:pw, 0:nsub, :], vb, self.VT)

            def drive(ga, gb, ratio):
                la, lb = ga is not None, gb is not None
                while la or lb:
                    if la:
                        try:
                            next(ga)
                        except StopIteration:
                            la = False
                    for _ in range(ratio):
                        if lb:
                            try:
                                next(gb)
                            except StopIteration:
                                lb = False

            n = len(tiles)
            S1(0)
            for i in range(n):
                drive(S2(i), S3b(i - 1) if (layer == 0 and i > 0) else None, 2)
                if i + 1 < n:
                    S1(i + 1)
                S3a(i)
            if layer == 0:
                drive(None, S3b(n - 1), 1)
            outs = [self.H1, self.QT, self.KT, self.VT] if layer == 0 else [self.y[slot]]
            k.barrier(outs)

    def phase_d(self):
        k, L = self.k, self.L
        k.push_phase()
        with k.phase:
            nkb = len(self.chunks)
            KTs = k.sb([128, 2, L], BF16, "dK")
            Vs = k.sb([128, nkb, 256], BF16, "dV")
            QTt = [k.sb([128, 512], BF16, "dQ") for _ in range(2)]
            NP = 6
            Pt = [k.sb([128, 512], BF16, "dP") for _ in range(NP)]
            sa = [k.sb([128, 512], BF16, "dsa") for _ in range(2)]
            sb_ = [k.sb([128, 512], BF16, "dsb") for _ in range(2)]
            sc = [k.sb([128, 512], BF16, "dsc") for _ in range(2)]
            se = [k.sb([128, 512], BF16, "dse") for _ in range(2)]
            rden = k.sb([128, 512], F32, "drden")
            ob = [k.sb([128, 512], BF16, "dob") for _ in range(2)]
            NSB = 4
            psS = [k.ps([128, 512], F32, "dpsS") for _ in range(NSB)]
            psO = [k.ps([128, 512], F32, "dpsO") for _ in range(2)]
            psD = [k.ps([128, 512], F32, "dpsD") for _ in range(2)]
            nsplit = 4
            step = (L + nsplit - 1) // nsplit
            for i in range(nsplit):
                a, b_ = i * step, min(L, (i + 1) * step)
                k.dma("sp", KTs[:, :, a:b_], self.KT.t[:, :, a:b_].rearrange("g p t -> p g t"), self.KT, KTs)
            k.op("pool", mset(Vs[:, 0, :], 0.0), [], [Vs])
            k.dma("sp", Vs[0:NMETA, 0, :], self.VT.t[0:NMETA, :], self.VT, Vs)
            nreal = nkb - 1
            assert nreal % 8 == 0
            for b0 in range(0, nreal, 16):
                nb = min(16, nreal - b0)
                k.dma("sp", Vs[:, 1 + b0:1 + b0 + nb, :], self.VT.t[NMETA + b0 * 128:NMETA + (b0 + nb) * 128, :].rearrange("(b p) f -> p b f", p=128), self.VT, Vs)
            scale = 128.0 ** -0.5
            n = 0
            for h in range(8):
                kv = h // 4
                for ti, (t0, w) in enumerate(self.tiles):
                    Q, po, pd, O = QTt[n % 2], psO[n % 2], psD[n % 2], ob[n % 2]
                    k.dma("sp", Q[:, 0:w], self.QT.t[h, :, t0:t0 + w], self.QT, Q)

                    def smm(kb):
                        k0, kw = self.chunks[kb]
                        ps = psS[kb % NSB]
                        k.op("pe", mm(ps[0:kw, 0:w], KTs[:, kv, k0:k0 + kw], Q[:, 0:w]), [KTs, Q], [ps])

                    def odm(kb):
                        k0, kw = self.chunks[kb]
                        ps, P = psS[kb % NSB], Pt[kb % NP]
                        k.op("act", act(P[0:kw, 0:w], ps[0:kw, 0:w], AF.Exp, scale=scale), [ps], [P])
                        k.op("pe", mm(po[:, 0:w], Vs[0:kw, kb, kv * 128:(kv + 1) * 128], P[0:kw, 0:w], kb == 0, kb == nkb - 1), [Vs, P], [po])
                        if kb == 0:
                            k.op("pe", mm(pd[:, 0:w], self.onesb[0:kw, :], P[0:kw, 0:w], True, False), [P, self.onesb], [pd])
                            return
                        r8 = (kb - 1) % 8
                        q2, o2 = ((kb - 1) // 4) % 2, ((kb - 1) // 8) % 2
                        Pp = Pt[(kb - 1) % NP]
                        if r8 % 4 == 1:
                            k.op("dve", tt(sa[q2][:, 0:w], Pp[:, 0:w], P[:, 0:w], ALU.add), [Pp, P], [sa[q2]])
                        elif r8 % 4 == 3:
                            k.op("pool", tt(sb_[q2][:, 0:w], Pp[:, 0:w], P[:, 0:w], ALU.add), [Pp, P], [sb_[q2]])
                            k.op("dve", tt(sc[q2][:, 0:w], sa[q2][:, 0:w], sb_[q2][:, 0:w], ALU.add), [sa[q2], sb_[q2]], [sc[q2]])
                            if r8 == 7:
                                k.op("dve", tt(se[o2][:, 0:w], sc[0][:, 0:w], sc[1][:, 0:w], ALU.add), [sc[0], sc[1]], [se[o2]])
                                pend.append((o2, kb == nkb - 1))
                        if r8 == 6 and pend:
                            flush()
                    pend = []

                    def flush():
                        qd_, last_ = pend.pop(0)
                        k.op("pe", mm(pd[:, 0:w], self.onesb[:], se[qd_][:, 0:w], False, last_), [se[qd_], self.onesb], [pd])
                    smm(0)
                    smm(1)
                    for kb in range(nkb):
                        if kb + 2 < nkb:
                            smm(kb + 2)
                        odm(kb)
                    while pend:
                        flush()
                    k.op("act", act(rden[:, 0:w], pd[:, 0:w], AF.Ln), [pd], [rden])
                    k.op("act", act(rden[:, 0:w], rden[:, 0:w], AF.Exp, scale=-1.0), [rden], [rden])
                    k.op("dve", tt(O[:, 0:w], po[:, 0:w], rden[:, 0:w], ALU.mult), [po, rden], [O])
                    k.dma("pool", self.AT.t[h, :, t0:t0 + w], O[:, 0:w], O, self.AT)
                    n += 1
            k.barrier([self.AT])

    def build(self):
        k = self.k
        with self.stack:
            self.declare()
            self.load_consts()
            self.prologue()
            for slot in range(self.NSLOT):
                self.phase_a(slot)
                if self.stop_after == "a":
                    continue
                self.phase_b12()
                if self.stop_after == "b2":
                    continue
                self.phase_b3()
                self.phase_b4()
                if self.stop_after == "b4":
                    continue
                self.phase_ce(slot, 0)
                if self.stop_after == "c":
                    continue
                self.phase_d()
                if self.stop_after == "d":
                    continue
                self.phase_ce(slot, 1)
        return self.nc


def host_consts(S):
    L = S + NMETA
    ident = np.eye(128, dtype=np.float32)
    i = np.arange(128)[:, None]
    j = np.arange(128)[None, :]
    masks = np.zeros((128, 6, 128), np.float32)
    masks[:, 0, :] = np.where(i > j, 0.0, NEG)
    masks[:, 1, :] = np.where(i < j, 0.0, NEG)
    masks[:, 2, :] = (i <= j).astype(np.float32)
    masks[:, 3, :] = (i >= j).astype(np.float32)
    masks[:, 4, :] = (i < NMETA).astype(np.float32)
    perm = np.zeros((128, 128), np.float32)
    perm[(np.arange(128) + 64) % 128, np.arange(128)] = 1.0
    rows = S // 64
    row = np.repeat(np.arange(rows), 64).astype(np.float32)
    col = np.tile(np.arange(64), rows).astype(np.float32)
    freqs = (10000.0 ** (-(np.arange(32, dtype=np.float32) / 32))).astype(np.float32)
    ang = np.concatenate([row[:, None] * freqs, col[:, None] * freqs], axis=-1)
    ang = np.concatenate([np.zeros((NMETA, 64), np.float32), ang], axis=0)
    cos, sin = np.cos(ang).T, np.sin(ang).T
    rope = np.zeros((2, 128, L), np.float32)
    rope[0, :64], rope[0, 64:] = cos, cos
    rope[1, :64], rope[1, 64:] = -sin, sin
    return dict(c_ident=ident, c_masks=masks, c_perm=perm, c_rope=rope)


def host_weights(p):
    f = lambda a: np.ascontiguousarray(np.asarray(a, dtype=np.float32))
    out = {}
    out["meta"] = f(p["meta_tokens"])
    out["g_mix"] = f(np.asarray(p["mix_norm"]).reshape(2, 8, 128).transpose(2, 0, 1))
    out["g_mlp"] = f(np.asarray(p["mlp_norm"]).reshape(2, 8, 128).transpose(2, 0, 1))
    out["w_in_even"] = f(p["w_in_even"][0])
    out["pool_w"] = f(p["pool_w"][0])
    out["pool_scale"] = f(np.asarray(p["pool_scale"][0]).reshape(4, 128).T)
    out["conv_w"] = f(np.asarray(p["conv_qkv"][0]).reshape(7, 12, 128).transpose(2, 1, 0))
    out["a_log"] = f(np.broadcast_to(np.asarray(p["a_log"][0]).reshape(1, 8), (128, 8)))
    out["dt_bias"] = f(np.broadcast_to(np.asarray(p["dt_bias"][0]).reshape(1, 8), (128, 8)))
    out["delta_norm"] = f(np.asarray(p["delta_norm"][0]).reshape(128, 1))
    out["w_out_even"] = f(p["w_out_even"][0])
    out["w_in_odd"] = f(p["w_in_odd"][0])
    out["qk_norm"] = f(np.stack([np.asarray(p["q_norm"][0]), np.asarray(p["k_norm"][0])], axis=1))
    out["w_out_odd"] = f(p["w_out_odd"][0])
    out["w_mlp_in"] = f(p["w_mlp_in"])
    out["w_mlp_out"] = f(p["w_mlp_out"])
    return out


_CACHE = {}


def run(seqs, params, S, n_cores, debug=False, stop_after=None, trace=False):
    nslot = (len(seqs) + n_cores - 1) // n_cores
    prog = Prog(S, nslot, debug=debug, stop_after=stop_after)
    nc = prog.build()
    base = dict(host_weights(params))
    base.update(host_consts(S))
    in_maps = []
    zero = np.zeros((S, D), np.float32)
    for c in range(n_cores):
        m = dict(base)
        for s in range(nslot):
            i = s * n_cores + c
            m["x%d" % s] = np.ascontiguousarray(seqs[i], dtype=np.float32) if i < len(seqs) else zero
        in_maps.append(m)
    res = run_bass_kernel_spmd(nc, in_maps, core_ids=list(range(n_cores)), trace=trace)
    return res, nslot, prog


def kernel(x_prompt, x_sample, **params):
    x_prompt = np.asarray(x_prompt)
    x_sample = np.asarray(x_sample)
    S = x_prompt.shape[1]
    seqs = [x_prompt[i] for i in range(x_prompt.shape[0])] + [x_sample[i] for i in range(x_sample.shape[0])]
    n_cores = 8
    res, nslot, _ = run(seqs, params, S, n_cores)
    outs = []
    for i in range(len(seqs)):
        c, s = i % n_cores, i // n_cores
        outs.append(np.asarray(res.results[c]["y%d" % s], dtype=np.float32))
    nb = x_prompt.shape[0]
    return np.stack(outs[:nb]), np.stack(outs[nb:])
```

```python
import numpy as np
from contextlib import ExitStack
import concourse.bass as bass
import concourse.mybir as mybir
from concourse.bass_utils import run_bass_kernel_spmd

F32 = mybir.dt.float32
BF16 = mybir.dt.bfloat16
AF = mybir.ActivationFunctionType
ALU = mybir.AluOpType

D = 1024
NMETA = 16
EPS = 1e-6
EVEN_IN = 2576
ODD_IN = 1536
DFF = 4096
NEG = -30000.0


class Buf:
    def __init__(self, t, name):
        self.t = t
        self.name = name
        self.lastw = {}
        self.wkind = None
        self.readers = {}
        self.psum = False

    def __getitem__(self, idx):
        return self.t[idx]


class Eng:
    def __init__(self, name, h, sem):
        self.name = name
        self.h = h
        self.sem = sem
        self.count = 0
        self.waited = {}
        self.dsems = []
        self.dnext = 0


NDSEM = 16


class KB:
    def __init__(self, nc, stack):
        self.nc = nc
        self.stack = stack
        self.phase = None
        self.e = {}
        for name, h in (("pe", nc.tensor), ("act", nc.scalar), ("dve", nc.vector), ("pool", nc.gpsimd), ("sp", nc.sync)):
            sem = stack.enter_context(nc.semaphore("s_" + name))
            self.e[name] = Eng(name, h, sem)
        for q in ("sp", "pool"):
            self.e[q].dsems = [[stack.enter_context(nc.semaphore("d_%s%d" % (q, i))), 0] for i in range(NDSEM)]
        self.nbuf = 0
        self.ninst = 0

    def push_phase(self):
        self.phase = ExitStack()
        return self.phase

    def sb(self, shape, dtype, name=None, persistent=False):
        self.nbuf += 1
        name = (name or "b") + "_%d" % self.nbuf
        st = self.stack if persistent else self.phase
        t = st.enter_context(self.nc.sbuf_tensor(name, list(shape), dtype))
        return Buf(t, name)

    def ps(self, shape, dtype, name=None):
        self.nbuf += 1
        name = (name or "p") + "_%d" % self.nbuf
        t = self.phase.enter_context(self.nc.psum_tensor(name, list(shape), dtype))
        b = Buf(t, name)
        b.psum = True
        return b

    def dram(self, name, shape, dtype, kind="Internal"):
        t = self.nc.dram_tensor(name, list(shape), dtype, kind=kind).ap()
        return Buf(t, name)

    def _wait(self, eng, deps):
        need = {}
        for s, v in deps:
            if need.get(s, 0) < v:
                need[s] = v
        for s, v in need.items():
            if eng.waited.get(s, 0) >= v:
                continue
            eng.h.wait_ge(s, v)
            eng.waited[s] = v

    def op(self, en, fns, reads=(), writes=()):
        eng = self.e[en]
        deps = []
        for b in reads:
            for s, v in b.lastw.items():
                if not (en == "pe" and s is eng.sem):
                    deps.append((s, v))
            if b.psum:
                for s, v in b.readers.items():
                    if s is not eng.sem:
                        deps.append((s, v))
        for b in writes:
            for s, v in b.lastw.items():
                if not (en == "pe" and s is eng.sem):
                    deps.append((s, v))
            for s, v in b.readers.items():
                if not (en == "pe" and s is eng.sem):
                    deps.append((s, v))
        self._wait(eng, deps)
        if not isinstance(fns, (list, tuple)):
            fns = [fns]
        ins = None
        for f in fns:
            ins = f(eng.h)
            self.ninst += 1
        eng.count += 1
        ins.then_inc(eng.sem, 1)
        for b in reads:
            if b.readers.get(eng.sem, 0) < eng.count:
                b.readers[eng.sem] = eng.count
        for b in writes:
            b.lastw = {eng.sem: eng.count}
            b.wkind = "eng"
            b.readers = {}

    def dma(self, q, out_ap, in_ap, src, dst, **kw):
        eng = self.e[q]
        slot = eng.dsems[eng.dnext % NDSEM]
        eng.dnext += 1
        sem, prev = slot
        deps = list(src.lastw.items())
        parallel = dst.wkind == "dma" and not dst.readers
        if not parallel:
            deps.extend(dst.lastw.items())
            deps.extend(dst.readers.items())
        if prev:
            deps.append((sem, prev))
        self._wait(eng, deps)
        ins = eng.h.dma_start(out=out_ap, in_=in_ap, **kw)
        self.ninst += 1
        val = prev + 16
        slot[1] = val
        ins.then_inc(sem, 16)
        if src.readers.get(sem, 0) < val:
            src.readers[sem] = val
        if parallel:
            dst.lastw[sem] = val
        else:
            dst.lastw = {sem: val}
        dst.wkind = "dma"
        dst.readers = {}

    def barrier(self, bufs):
        deps = []
        for b in bufs:
            deps.extend(b.lastw.items())
            deps.extend(b.readers.items())
        for eng in self.e.values():
            if eng.count:
                deps.append((eng.sem, eng.count))
            for sem, val in eng.dsems:
                if val:
                    deps.append((sem, val))
        for eng in self.e.values():
            self._wait(eng, deps)


def mm(out, lhsT, rhs, start=True, stop=True):
    return lambda e: e.matmul(out, lhsT, rhs, start=start, stop=stop)


def tr(out, in_, ident):
    return lambda e: e.transpose(out, in_, ident)


def act(out, in_, func, bias=None, scale=None, accum_out=None):
    kw = {}
    if bias is not None:
        kw["bias"] = bias
    if scale is not None:
        kw["scale"] = scale
    if accum_out is not None:
        kw["accum_out"] = accum_out
    return lambda e: e.activation(out, in_, func, **kw)


def tt(out, a, b, op):
    return lambda e: e.tensor_tensor(out, a, b, op)


def tsc(out, a, s1, op0, s2=None, op1=None):
    if op1 is None:
        return lambda e: e.tensor_scalar(out, a, s1, None, op0)
    return lambda e: e.tensor_scalar(out, a, s1, s2, op0, op1)


def stt(out, a, s, b, op0, op1):
    return lambda e: e.scalar_tensor_tensor(out, a, s, b, op0, op1)


def cp(out, in_):
    return lambda e: e.tensor_copy(out, in_)


def acp(out, in_):
    return lambda e: e.activation(out, in_, AF.Copy)


def rcp(out, in_):
    return lambda e: e.reciprocal(out, in_)


def mset(ap, v):
    return lambda e: e.memset(ap, v)


class Prog:
    def __init__(self, S, NSLOT, debug=False, stop_after=None):
        self.S = S
        self.L = S + NMETA
        self.NSLOT = NSLOT
        self.debug = debug
        self.stop_after = stop_after
        self.nc = bass.Bass("TRN2", target_bir_lowering=False)
        self.stack = ExitStack()
        self.k = KB(self.nc, self.stack)
        self.tiles = [(0, NMETA)] + [(NMETA + 512 * i, 512) for i in range(S // 512)]
        self.chunks = [(0, NMETA)] + [(NMETA + 128 * i, 128) for i in range(S // 128)]

    def declare(self):
        k, L, S = self.k, self.L, self.S
        ext = lambda n, s, dt=F32: k.dram(n, s, dt, kind="ExternalInput")
        self.x = [ext("x%d" % s, [S, D]) for s in range(self.NSLOT)]
        self.y = [k.dram("y%d" % s, [S, D], F32, kind="ExternalOutput") for s in range(self.NSLOT)]
        self.meta = ext("meta", [NMETA, D])
        self.g_mix = ext("g_mix", [128, 2, 8])
        self.g_mlp = ext("g_mlp", [128, 2, 8])
        self.w_in_even = ext("w_in_even", [D, EVEN_IN])
        self.pool_w = ext("pool_w", [4, 128, 128])
        self.pool_scale = ext("pool_scale", [128, 4])
        self.conv_w = ext("conv_w", [128, 12, 7])
        self.a_log = ext("a_log", [128, 8])
        self.dt_bias = ext("dt_bias", [128, 8])
        self.delta_norm = ext("delta_norm", [128, 1])
        self.w_out_even = ext("w_out_even", [D, D])
        self.w_in_odd = ext("w_in_odd", [D, ODD_IN])
        self.qk_norm = ext("qk_norm", [128, 2])
        self.w_out_odd = ext("w_out_odd", [D, D])
        self.w_mlp_in = ext("w_mlp_in", [2, D, DFF])
        self.w_mlp_out = ext("w_mlp_out", [2, DFF, D])
        self.c_ident = ext("c_ident", [128, 128])
        self.c_masks = ext("c_masks", [128, 6, 128])
        self.c_perm = ext("c_perm", [128, 128])
        self.c_rope = ext("c_rope", [2, 128, L])
        self.wt_in_even = k.dram("wt_in_even", [21, 128, 8 * 128], BF16)
        self.wt_out_even = k.dram("wt_out_even", [8, 128, 8 * 128], BF16)
        self.wt_in_odd = k.dram("wt_in_odd", [12, 128, 8 * 128], BF16)
        self.wt_out_odd = k.dram("wt_out_odd", [8, 128, 8 * 128], BF16)
        self.wt_mlp_in = [k.dram("wt_mlp_in%d" % i, [32, 128, 8 * 128], BF16) for i in range(2)]
        self.wt_mlp_out = [k.dram("wt_mlp_out%d" % i, [8, 128, 32 * 128], BF16) for i in range(2)]
        dbg = "ExternalOutput" if self.debug else "Internal"
        self.H0 = k.dram("H0", [8, 128, L], F32, kind=dbg)
        self.UP = k.dram("UP", [4, 128, L], F32, kind=dbg)
        self.UQ = k.dram("UQ", [12, 128, L], BF16, kind=dbg)
        self.UZ = k.dram("UZ", [4, 128, L], F32, kind=dbg)
        self.UBA = k.dram("UBA", [L, 16], F32, kind=dbg)
        self.QKVN = k.dram("QKVN", [12, 128, L], BF16, kind=dbg)
        self.YM = k.dram("YM", [8, 128, L], BF16, kind=dbg)
        self.OF = k.dram("OF", [2, 4, 128, L], F32, kind=dbg)
        self.QT = k.dram("QT", [8, 128, L], BF16, kind=dbg)
        self.KT = k.dram("KT", [2, 128, L], BF16, kind=dbg)
        self.VT = k.dram("VT", [L, 256], BF16, kind=dbg)
        self.AT = k.dram("AT", [8, 128, L], BF16, kind=dbg)
        self.H1 = k.dram("H1", [8, 128, L], F32, kind=dbg)

    def load_consts(self):
        k = self.k
        P = True
        self.ident = k.sb([128, 128], F32, "ident", P)
        self.identb = k.sb([128, 128], BF16, "identb", P)
        self.onesb = k.sb([128, 128], BF16, "onesb", P)
        self.gmix = k.sb([128, 2, 8], F32, "gmix", P)
        self.gmlp = k.sb([128, 2, 8], F32, "gmlp", P)
        self.epsc = k.sb([128, 1], F32, "epsc", P)
        self.eps128 = k.sb([128, 1], F32, "eps128", P)
        self.onec = k.sb([128, 1], F32, "onec", P)
        k.dma("sp", self.ident[:], self.c_ident[:, :], self.c_ident, self.ident)
        k.dma("sp", self.gmix[:], self.g_mix[:, :, :], self.g_mix, self.gmix)
        k.dma("sp", self.gmlp[:], self.g_mlp[:, :, :], self.g_mlp, self.gmlp)
        k.op("dve", cp(self.identb[:], self.ident[:]), [self.ident], [self.identb])
        k.op("dve", mset(self.onesb[:], 1.0), [], [self.onesb])
        k.op("dve", mset(self.epsc[:], EPS), [], [self.epsc])
        k.op("dve", mset(self.eps128[:], 128.0 * EPS), [], [self.eps128])
        k.op("dve", mset(self.onec[:], 1.0), [], [self.onec])

    def prologue(self):
        k = self.k
        k.push_phase()
        with k.phase:
            stf = [k.sb([128, DFF], F32, "wstf") for _ in range(2)]
            stb = [k.sb([128, DFF], BF16, "wstb") for _ in range(2)]
            jobs = []
            def add(src, src2d, ncols, nk, gain, dst):
                for kc in range(nk):
                    jobs.append((src, src2d, ncols, kc, gain, dst))
            add(self.w_in_even, self.w_in_even.t, EVEN_IN, 8, (self.gmix, 0), self.wt_in_even)
            add(self.w_out_even, self.w_out_even.t, D, 8, None, self.wt_out_even)
            add(self.w_in_odd, self.w_in_odd.t, ODD_IN, 8, (self.gmix, 1), self.wt_in_odd)
            add(self.w_out_odd, self.w_out_odd.t, D, 8, None, self.wt_out_odd)
            for i in range(2):
                add(self.w_mlp_in, self.w_mlp_in.t[i], DFF, 8, (self.gmlp, i), self.wt_mlp_in[i])
                add(self.w_mlp_out, self.w_mlp_out.t[i], D, 32, None, self.wt_mlp_out[i])
            for n, (src, src2d, ncols, kc, gain, dst) in enumerate(jobs):
                f, b = stf[n % 2], stb[n % 2]
                k.dma("sp", f[:, 0:ncols], src2d[kc * 128:(kc + 1) * 128, :], src, f)
                if gain is not None:
                    gb, gi = gain
                    k.op("dve", tsc(b[:, 0:ncols], f[:, 0:ncols], gb[:, gi, kc:kc + 1], ALU.mult), [f, gb], [b])
                else:
                    k.op("act", acp(b[:, 0:ncols], f[:, 0:ncols]), [f], [b])
                nm = (ncols + 127) // 128
                nfull = ncols // 128
                if nm > nfull:
                    k.op("pool", mset(b[:, ncols:nm * 128], 0.0), [], [b])
                dv = dst.t
                k.dma("pool", dv[0:nfull, :, kc * 128:(kc + 1) * 128].rearrange("m p c -> p m c"),
                      b[:, 0:nfull * 128].rearrange("p (m c) -> p m c", c=128), b, dst)
                if nm > nfull:
                    k.dma("pool", dv[nfull, :, kc * 128:(kc + 1) * 128], b[:, nfull * 128:nm * 128], b, dst)
            k.barrier([self.wt_in_even, self.wt_out_even, self.wt_in_odd, self.wt_out_odd] + self.wt_mlp_in + self.wt_mlp_out)

    def rms_to_bf16(self, hT, w, sq, psS, tmp, rstd, xn, engs=("dve", "pool")):
        k = self.k
        if isinstance(sq, Buf):
            for c in range(8):
                k.op("act", act(sq[:, c, 0:w], hT[:, c, 0:w], AF.Square), [hT], [sq])
            k.op("pe", [mm(psS[:, 0:w], self.onesb[:], sq[:, c, 0:w], c == 0, c == 7) for c in range(8)], [sq, self.onesb], [psS])
        else:
            for c in range(8):
                s_ = sq[c % len(sq)]
                k.op("act", act(s_[:, 0:w], hT[:, c, 0:w], AF.Square), [hT], [s_])
                k.op("pe", mm(psS[:, 0:w], self.onesb[:], s_[:, 0:w], c == 0, c == 7), [s_, self.onesb], [psS])
        k.op("act", act(tmp[:, 0:w], psS[:, 0:w], AF.Ln, bias=self.epsc[:, 0:1], scale=1.0 / D), [psS, self.epsc], [tmp])
        k.op("act", act(rstd[:, 0:w], tmp[:, 0:w], AF.Exp, scale=-0.5), [tmp], [rstd])
        for c in range(8):
            k.op(engs[c % len(engs)], tt(xn[:, c, 0:w], hT[:, c, 0:w], rstd[:, 0:w], ALU.mult), [hT, rstd], [xn])

    def phase_a(self, slot):
        k = self.k
        k.push_phase()
        with k.phase:
            xt = [k.sb([128, 4, D], F32, "xt") for _ in range(2)]
            hT = [k.sb([128, 8, 512], F32, "hT") for _ in range(2)]
            sq = k.sb([128, 8, 512], BF16, "sq")
            tmp = k.sb([128, 512], F32, "tmp")
            rstd = k.sb([128, 512], F32, "rstd")
            xn = [k.sb([128, 8, 512], BF16, "xn") for _ in range(2)]
            wb = [k.sb([128, 3, 8 * 128], BF16, "wb") for _ in range(3)]
            of = [k.sb([128, 4, 512], F32, "of") for _ in range(2)]
            ob = [k.sb([128, 4, 512], BF16, "ob") for _ in range(2)]
            oba = [k.sb([128, 4, 16], F32, "oba") for _ in range(2)]
            psT = [k.ps([128, 512], F32, "psT") for _ in range(2)]
            psS = k.ps([128, 512], F32, "psS")
            psU = [k.ps([128, 512], F32, "psU") for _ in range(3)]
            psB = k.ps([128, 4, 16], F32, "psB")
            def T(ti):
                t0, w = self.tiles[ti]
                X, H, XN = xt[ti % 2], hT[ti % 2], xn[ti % 2]
                nsub = (w + 127) // 128
                if ti == 0:
                    k.dma("sp", X[0:NMETA, 0, :], self.meta[:, :], self.meta, X)
                else:
                    r0 = t0 - NMETA
                    k.dma("sp", X[:, :, :], self.x[slot].t[r0:r0 + 512, :].rearrange("(j p) d -> p j d", p=128), self.x[slot], X)
                pw = min(w, 128)
                for c in range(8):
                    pt = psT[c % 2]
                    k.op("pe", [tr(pt[:, j * 128:j * 128 + pw], X[0:pw, j, c * 128:(c + 1) * 128], self.ident[0:pw, 0:pw]) for j in range(nsub)],
                         [X, self.ident], [pt])
                    k.op("dve", cp(H[:, c, 0:w], pt[:, 0:w]), [pt], [H])
                self.rms_to_bf16(H, w, sq, psS, tmp, rstd, XN)
                k.dma("pool", self.H0.t[:, :, t0:t0 + w].rearrange("c p t -> p c t"), H[:, :, 0:w], H, self.H0)

            def P(ti):
                t0, w = self.tiles[ti]
                X, H, XN = xt[ti % 2], hT[ti % 2], xn[ti % 2]
                nsub = (w + 127) // 128
                pw = min(w, 128)
                for g in range(7):
                    W = wb[g % 3]
                    k.dma("sp", W[:, :, :], self.wt_in_even.t[3 * g:3 * g + 3, :, :].rearrange("m p c -> p m c"), self.wt_in_even, W)
                    for mi in range(3):
                        m = 3 * g + mi
                        if m < 20:
                            pu = psU[m % 3]
                            k.op("pe", [mm(pu[:, 0:w], W[:, mi, c * 128:(c + 1) * 128], XN[:, c, 0:w], c == 0, c == 7) for c in range(8)], [W, XN], [pu])
                            grp = m // 4
                            if grp == 0 or grp == 4:
                                o = of[0 if grp == 0 else 1]
                                dst, d0 = (self.UP, 0) if grp == 0 else (self.UZ, 0)
                            else:
                                o = ob[grp % 2]
                                dst, d0 = self.UQ, (grp - 1) * 4
                            k.op("act" if m % 2 else "dve", (acp if m % 2 else cp)(o[:, m % 4, 0:w], pu[:, 0:w]), [pu], [o])
                            if m % 4 == 3:
                                k.dma("pool", dst.t[d0:d0 + 4, :, t0:t0 + w].rearrange("g p t -> p g t"), o[:, :, 0:w], o, dst)
                        else:
                            O = oba[ti % 2]
                            k.op("pe", [mm(psB[0:pw, j, :], XN[:, c, j * 128:j * 128 + pw], W[:, mi, c * 128:c * 128 + 16], c == 0, c == 7)
                                        for j in range(nsub) for c in range(8)], [W, XN], [psB])
                            k.op("dve", cp(O[0:pw, 0:nsub, :], psB[0:pw, 0:nsub, :]), [psB], [O])
                            k.dma("pool", self.UBA.t[t0:t0 + w, :].rearrange("(j p) f -> p j f", p=pw), O[0:pw, 0:nsub, :], O, self.UBA)

            nt = len(self.tiles)
            T(0)
            for ti in range(nt):
                if ti + 1 < nt:
                    T(ti + 1)
                P(ti)
            k.barrier([self.H0, self.UP, self.UQ, self.UZ, self.UBA])


    def _b1_gen(self):
        k, L = self.k, self.L
        if True:
            up = [k.sb([128, 4, 528], F32, "up") for _ in range(2)]
            ta = [k.sb([128, 528], F32, "pta") for _ in range(4)]
            tb = [k.sb([128, 528], F32, "ptb") for _ in range(4)]
            dd = [k.sb([128, 4, 512], BF16, "pd") for _ in range(2)]
            yb = [k.sb([128, 4, 512], BF16, "pyb") for _ in range(2)]
            pwf = k.sb([128, 4, 128], F32, "pwf")
            pwb = k.sb([128, 4, 128], BF16, "pwb")
            psc = k.sb([128, 4], F32, "psc")
            psY = [k.ps([128, 512], F32, "psY") for _ in range(2)]
            k.dma("sp", pwf[:], self.pool_w.t.rearrange("g c d -> c g d"), self.pool_w, pwf)
            k.dma("sp", psc[:], self.pool_scale[:, :], self.pool_scale, psc)
            k.op("dve", cp(pwb[:], pwf[:]), [pwf], [pwb])
            tilesB = [(t0, min(512, L - t0)) for t0 in range(0, L, 512)]
            for ti, (t0, w) in enumerate(tilesB):
                U, Dd, Y = up[ti % 2], dd[ti % 2], yb[ti % 2]
                lo, hi = t0 - 8, t0 + w + 8
                clo, chi = max(lo, 0), min(hi, L)
                if clo != lo or chi != hi:
                    k.op("dve", mset(U[:], 0.0), [], [U])
                k.dma("sp", U[:, :, clo - lo:chi - lo], self.UP.t[:, :, clo:chi].rearrange("g p t -> p g t"), self.UP, U)
                n = w + 16
                for g in range(4):
                    W = 2 << g
                    eng = "dve" if g < 2 else "pool"
                    A, B = ta[g], tb[g]
                    k.op(eng, tt(A[:, 1:n], U[:, g, 0:n - 1], U[:, g, 1:n], ALU.add), [U], [A])
                    cur, oth, lo_b, hi_b, sh = A, B, 1, n, 1
                    for lvl in range(g):
                        nlo, nhi = lo_b + sh, hi_b - sh
                        k.op(eng, tt(oth[:, nlo:nhi], cur[:, nlo - sh:nhi - sh], cur[:, nlo + sh:nhi + sh], ALU.add), [cur], [oth])
                        cur, oth, lo_b, hi_b, sh = oth, cur, nlo, nhi, sh * 2
                    assert lo_b <= 8 and hi_b >= w + 8
                    k.op("dve", stt(Dd[:, g, 0:w], cur[:, 8:8 + w], 1.0 / W, U[:, g, 8:8 + w], ALU.mult, ALU.subtract), [cur, U], [Dd])
                    fix = []
                    for t in range(t0, t0 + w):
                        lo_t, hi_t = max(t - W // 2, 0), min(t + (W - 1 - W // 2), L - 1)
                        cnt = hi_t - lo_t + 1
                        if cnt != W:
                            fix.append((t, cnt))
                    for (t, cnt) in fix:
                        b = t - t0
                        k.op("dve", stt(Dd[:, g, b:b + 1], cur[:, 8 + b:9 + b], 1.0 / cnt, U[:, g, 8 + b:9 + b], ALU.mult, ALU.subtract), [cur, U], [Dd])
                    py = psY[g % 2]
                    k.op("pe", mm(py[:, 0:w], pwb[:, g, :], Dd[:, g, 0:w]), [pwb, Dd], [py])
                    k.op("act", act(Y[:, g, 0:w], py[:, 0:w], AF.Copy, scale=psc[:, g:g + 1]), [py, psc], [Y])
                k.dma("pool", self.YM.t[0:4, :, t0:t0 + w].rearrange("g p t -> p g t"), Y[:, :, 0:w], Y, self.YM)
                yield

    def _b2_gen(self):
        k, L = self.k, self.L
        if True:
            uq = [k.sb([128, 12, 518], BF16, "uq") for _ in range(2)]
            cwf = k.sb([128, 12, 7], F32, "cwf")
            dg = k.sb([128, 84, 128], BF16, "dg")
            cs2 = [k.sb([128, 12, 512], F32, "cs") for _ in range(2)]
            sq = [k.sb([128, 512], BF16, "sq2") for _ in range(2)]
            tmp2 = [k.sb([128, 8, 512], F32, "tmp2") for _ in range(2)]
            ob = [k.sb([128, 12, 512], BF16, "ob2") for _ in range(2)]
            psC = [k.ps([128, 512], F32, "psC") for _ in range(3)]
            psN = [k.ps([128, 512], F32, "psN") for _ in range(2)]
            k.dma("sp", cwf[:], self.conv_w[:, :, :], self.conv_w, cwf)
            for ch in range(12):
                for j in range(7):
                    k.op("dve" if (ch + j) % 2 else "pool", tsc(dg[:, ch * 7 + j, :], self.identb[:], cwf[:, ch, j:j + 1], ALU.mult), [self.identb, cwf], [dg])
            for ti, (t0, w) in enumerate(self.tiles):
                U, O = uq[ti % 2], ob[ti % 2]
                cs, tmp = cs2[ti % 2], tmp2[ti % 2]
                rs = tmp
                lo, hi = t0 - 3, t0 + w + 3
                clo, chi = max(lo, 0), min(hi, L)
                if clo != lo or chi != hi:
                    k.op("pool", mset(U[:], 0.0), [], [U])
                k.dma("sp", U[:, :, clo - lo:chi - lo], self.UQ.t[:, :, clo:chi].rearrange("g p t -> p g t"), self.UQ, U)
                for ch in range(12):
                    pc = psC[ch % 3]
                    k.op("pe", [mm(pc[:, 0:w], dg[:, ch * 7 + j, :], U[:, ch, j:j + w], j == 0, j == 6) for j in range(7)], [dg, U], [pc])
                    k.op("act", act(cs[:, ch, 0:w], pc[:, 0:w], AF.Silu), [pc], [cs])
                for ch in range(8):
                    s = sq[ch % 2]
                    pn = psN[ch % 2]
                    k.op("dve", tt(s[:, 0:w], cs[:, ch, 0:w], cs[:, ch, 0:w], ALU.mult), [cs], [s])
                    k.op("pe", mm(pn[:, 0:w], self.onesb[:], s[:, 0:w]), [s, self.onesb], [pn])
                    if ch < 4:
                        k.op("act", act(tmp[:, ch, 0:w], pn[:, 0:w], AF.Ln, bias=self.eps128[:, 0:1], scale=128.0), [pn, self.eps128], [tmp])
                    else:
                        k.op("act", act(tmp[:, ch, 0:w], pn[:, 0:w], AF.Ln, bias=self.epsc[:, 0:1], scale=1.0), [pn, self.epsc], [tmp])
                k.op("act", act(rs[:, :, 0:w], tmp[:, :, 0:w], AF.Exp, scale=-0.5), [tmp], [rs])
                for ch in range(8):
                    k.op("pool" if ch % 2 else "dve", tt(O[:, ch, 0:w], cs[:, ch, 0:w], rs[:, ch, 0:w], ALU.mult), [cs, rs], [O])
                k.op("act", acp(O[:, 8:12, 0:w], cs[:, 8:12, 0:w]), [cs], [O])
                k.dma("pool", self.QKVN.t[:, :, t0:t0 + w].rearrange("g p t -> p g t"), O[:, :, 0:w], O, self.QKVN)
                yield

    def phase_b12(self):
        k = self.k
        k.push_phase()
        with k.phase:
            gens = [self._b1_gen(), self._b2_gen()]
            live = [True, True]
            while any(live):
                for i in range(2):
                    if live[i]:
                        try:
                            next(gens[i])
                        except StopIteration:
                            live[i] = False
            k.barrier([self.YM, self.QKVN])

    def phase_b3(self):
        k, L = self.k, self.L
        k.push_phase()
        with k.phase:
            NCH = len(self.chunks)
            msk = k.sb([128, 6, 128], F32, "msk")
            k.dma("sp", msk[:], self.c_masks[:, :, :], self.c_masks, msk)
            mskb = k.sb([128, 6, 128], BF16, "mskb")
            k.op("dve", cp(mskb[:], msk[:]), [msk], [mskb])
            I4 = k.sb([128, 4, 128], F32, "I4")
            negmask4 = [k.sb([128, 4, 128], F32, "negm4") for _ in range(2)]
            Um4 = [k.sb([128, 4, 128], F32, "Um4") for _ in range(2)]
            for h in range(4):
                k.op("pool", cp(I4[:, h, :], self.ident[:]), [self.ident], [I4])
                for d in range(2):
                    k.op("pool", cp(negmask4[d][:, h, :], msk[:, d, :]), [msk], [negmask4[d]])
                    k.op("pool", cp(Um4[d][:, h, :], msk[:, 2 + d, :]), [msk], [Um4[d]])
            alog = k.sb([128, 8], F32, "alog")
            dtb = k.sb([128, 8], F32, "dtb")
            nea = k.sb([128, 8], F32, "nea")
            k.dma("sp", alog[:], self.a_log[:, :], self.a_log, alog)
            k.dma("sp", dtb[:], self.dt_bias[:, :], self.dt_bias, dtb)
            k.op("act", act(nea[:], alog[:], AF.Exp), [alog], [nea])
            k.op("dve", tsc(nea[:], nea[:], -1.0, ALU.mult), [nea], [nea])
            BA = k.sb([128, NCH, 16], F32, "gBA")
            k.op("pool", mset(BA[:, 0, :], 0.0), [], [BA])
            k.dma("sp", BA[0:NMETA, 0, :], self.UBA.t[0:NMETA, :], self.UBA, BA)
            k.dma("sp", BA[:, 1:NCH, :], self.UBA.t[NMETA:L, :].rearrange("(c p) f -> p c f", p=128), self.UBA, BA)
            sm = {n_: k.sb([128, NCH, 8], F32, "g" + n_) for n_ in ("beta", "nbeta", "sp", "g", "gc", "gl", "egc", "kds", "egl", "bw")}
            ghl = k.sb([128, 2, NCH, 8], BF16, "ghl")
            ghf = k.sb([128, 2, NCH, 8], F32, "ghf")
            bc8 = lambda ap: ap.unsqueeze(1).broadcast_to([128, NCH, 8])
            k.op("act", act(sm["beta"][:], BA[:, :, 0:8], AF.Exp, scale=-1.0), [BA], [sm["beta"]])
            k.op("dve", tsc(sm["beta"][:], sm["beta"][:], 1.0, ALU.add), [sm["beta"]], [sm["beta"]])
            k.op("dve", rcp(sm["beta"][:], sm["beta"][:]), [sm["beta"]], [sm["beta"]])
            k.op("dve", tt(sm["sp"][:], BA[:, :, 8:16], bc8(dtb[:]), ALU.add), [BA, dtb], [sm["sp"]])
            k.op("act", act(sm["sp"][:], sm["sp"][:], AF.Exp), [sm["sp"]], [sm["sp"]])
            k.op("act", act(sm["sp"][:], sm["sp"][:], AF.Ln, bias=self.onec[:, 0:1]), [sm["sp"], self.onec], [sm["sp"]])
            k.op("dve", tt(sm["g"][:], sm["sp"][:], bc8(nea[:]), ALU.mult), [sm["sp"], nea], [sm["g"]])
            k.op("dve", tsc(sm["beta"][:, 0, :], sm["beta"][:, 0, :], msk[:, 4, 0:1], ALU.mult), [sm["beta"], msk], [sm["beta"]])
            k.op("dve", tsc(sm["g"][:, 0, :], sm["g"][:, 0, :], msk[:, 4, 0:1], ALU.mult), [sm["g"], msk], [sm["g"]])
            k.op("dve", tsc(sm["nbeta"][:], sm["beta"][:], -1.0, ALU.mult), [sm["beta"]], [sm["nbeta"]])
            k.op("dve", cp(ghl[:, 0], sm["g"][:]), [sm["g"]], [ghl])
            k.op("dve", cp(ghf[:, 0], ghl[:, 0]), [ghl], [ghf])
            k.op("dve", tt(ghf[:, 1], sm["g"][:], ghf[:, 0], ALU.subtract), [sm["g"], ghf], [ghf])
            k.op("dve", cp(ghl[:, 1], ghf[:, 1]), [ghf], [ghl])
            k.op("dve", cp(ghf[:, 1], ghl[:, 1]), [ghl], [ghf])
            pA = [k.ps([128, 4, 128], F32, "pA") for _ in range(2)]
            pB = [k.ps([128, 4, 128], F32, "pB") for _ in range(2)]
            pC = [k.ps([128, 4, 128], F32, "pC") for _ in range(2)]
            pT = [k.ps([128, 2, 4, 128], BF16, "pT") for _ in range(2)]
            NS = NCH * 4
            for d in range(2):
                dsl = slice(d * 4, d * 4 + 4)
                fa = pA[d].t[:].rearrange("p h j -> p (h j)")
                fb = pB[d].t[:].rearrange("p h j -> p (h j)")
                o3 = lambda f: f[:, 0:NS].rearrange("p (c h) -> p c h", h=4)
                k.op("pe", [mm(o3(fa), mskb[:, 2 + d, :], ghl[:, 0, :, dsl], True, False), mm(o3(fa), mskb[:, 2 + d, :], ghl[:, 1, :, dsl], False, True)],
                     [mskb, ghl], [pA[d]])
                k.op("pe", [mm(o3(fb), self.onesb[:], ghl[:, 0, :, dsl], True, False), mm(o3(fb), self.onesb[:], ghl[:, 1, :, dsl], False, True)],
                     [self.onesb, ghl], [pB[d]])
                k.op("dve", cp(sm["gc"][:, :, dsl], o3(fa)), [pA[d]], [sm["gc"]])
                k.op("dve", cp(sm["gl"][:, :, dsl], o3(fb)), [pB[d]], [sm["gl"]])
            k.op("act", act(sm["egc"][:], sm["gc"][:], AF.Exp), [sm["gc"]], [sm["egc"]])
            k.op("act", act(sm["egl"][:], sm["gl"][:], AF.Exp), [sm["gl"]], [sm["egl"]])
            k.op("dve", tt(sm["kds"][:], sm["gl"][:], sm["gc"][:], ALU.subtract), [sm["gl"], sm["gc"]], [sm["kds"]])
            k.op("act", act(sm["kds"][:], sm["kds"][:], AF.Exp), [sm["kds"]], [sm["kds"]])
            k.op("dve", tt(sm["bw"][:], sm["beta"][:], sm["egc"][:], ALU.mult), [sm["beta"], sm["egc"]], [sm["bw"]])

            def make_dir(d):
                dsl = slice(d * 4, d * 4 + 4)
                NB = 2
                dbl = lambda shape, dt, name: [k.sb(shape, dt, name + str(d)) for _ in range(NB)]
                qkv = dbl([128, 12, 128], BF16, "gqkv")
                Bm = dbl([128, 2, 4, 128], BF16, "gBm")
                arg = dbl([128, 4, 128], F32, "garg")
                Ds = dbl([128, 4, 128], F32, "gDs")
                Dsn = dbl([128, 4, 128], F32, "gDsn")
                Di = dbl([128, 4, 128], F32, "gDi")
                EGB = dbl([128, 4, 128], F32, "gEGB")
                qdT = dbl([128, 4, 128], BF16, "gqdT")
                Xa = dbl([128, 4, 128], BF16, "gXa")
                Xb = dbl([128, 4, 128], BF16, "gXb")
                XI = dbl([128, 4, 128], BF16, "gXI")
                Yb = dbl([128, 4, 128], BF16, "gYb")
                att = dbl([128, 4, 128], BF16, "gatt")
                YA = dbl([128, 2, 4, 128], BF16, "gYA")
                Qa = dbl([128, 4, 128], BF16, "gQa")
                Qc = dbl([128, 4, 128], BF16, "gQc")
                kvtok = dbl([128, 2, 4, 128], BF16, "gkvtok")
                vb = dbl([128, 4, 128], BF16, "gvb")
                kbg = dbl([128, 4, 128], BF16, "gkbg")
                kdec = dbl([128, 4, 128], BF16, "gkdec")
                nwcT = dbl([128, 4, 128], BF16, "gnwcT")
                vn = dbl([128, 4, 128], BF16, "gvn")
                osb = dbl([128, 4, 128], F32, "gosb")
                Sf = k.sb([128, 4, 128], F32, "gSf" + str(d))
                Sb = k.sb([128, 4, 128], BF16, "gSb" + str(d))
                k.op("dve", mset(Sf[:], 0.0), [], [Sf])
                k.op("pool", mset(Sb[:], 0.0), [], [Sb])
                A_, B_, C_, T_ = pA[d], pB[d], pC[d], pT[d]
                Qfin = {}
                bc = lambda ap: ap.unsqueeze(2).broadcast_to([128, 4, 128])

                def prep(ci, n):
                    t0, w = self.chunks[ci]
                    b = n % NB
                    Q_ = qkv[b]
                    if w < 128:
                        k.op("pool", mset(Q_[:], 0.0), [], [Q_])
                    k.dma("sp", Q_[:, :, 0:w], self.QKVN.t[:, :, t0:t0 + w].rearrange("g p t -> p g t"), self.QKVN, Q_)
                    for hl in range(2):
                        k.op("pool", tt(Bm[b][:, hl], Um4[d][:], bc(ghf[:, hl, ci, dsl]), ALU.mult), [Um4[d], ghf], [Bm[b]])
                    k.op("pe", [f for h in range(4) for f in (mm(C_[:, h, :], self.onesb[:], Bm[b][:, 0, h, :], True, False),
                                                              mm(C_[:, h, :], self.onesb[:], Bm[b][:, 1, h, :], False, True))], [self.onesb, Bm[b]], [C_])
                    k.op("pe", [mm(A_[:, h, :], Q_[:, 4 + h, :], Q_[:, 4 + h, :]) for h in range(4)], [Q_], [A_])
                    k.op("pe", [mm(B_[:, h, :], Q_[:, h, :], Q_[:, 4 + h, :]) for h in range(4)], [Q_], [B_])
                    k.op("pe", [tr(T_[:, 0, h, :], Q_[:, 4 + h, :], self.identb[:]) for h in range(4)] + [tr(T_[:, 1, h, :], Q_[:, 8 + h, :], self.identb[:]) for h in range(4)],
                         [Q_, self.identb], [T_])
                    k.op("dve", stt(arg[b][:], C_[:], -1.0, negmask4[d][:], ALU.mult, ALU.add), [C_, negmask4[d]], [arg[b]])
                    k.op("act", act(EGB[b][:], C_[:], AF.Exp), [C_], [EGB[b]])
                    k.op("dve", tt(arg[b][:], arg[b][:], bc(sm["gc"][:, ci, dsl]), ALU.add), [arg[b], sm["gc"]], [arg[b]])
                    k.op("act", act(Ds[b][:], arg[b][:], AF.Exp), [arg[b]], [Ds[b]])
                    k.op("dve", cp(kvtok[b][:], T_[:]), [T_], [kvtok[b]])
                    yield
                    k.op("pool", tt(Dsn[b][:], Ds[b][:], bc(sm["nbeta"][:, ci, dsl]), ALU.mult), [Ds[b], sm["nbeta"]], [Dsn[b]])
                    k.op("pool", tt(Di[b][:], Ds[b][:], I4[:], ALU.add), [Ds[b], I4], [Di[b]])
                    k.op("dve", tt(Xa[b][:], A_[:], Dsn[b][:], ALU.mult), [A_, Dsn[b]], [Xa[b]])
                    k.op("dve", tt(att[b][:], B_[:], Di[b][:], ALU.mult), [B_, Di[b]], [att[b]])
                    k.op("pe", [tr(T_[:, 0, h, :], Xa[b][:, h, :], self.identb[:]) for h in range(4)] + [tr(T_[:, 1, h, :], att[b][:, h, :], self.identb[:]) for h in range(4)],
                         [Xa[b], att[b], self.identb], [T_])
                    k.op("act", acp(YA[b][:], T_[:]), [T_], [YA[b]])
                    k.op("dve", tt(Qa[b][:], I4[:], YA[b][:, 0], ALU.add), [I4, YA[b]], [Qa[b]])
                    k.op("pool", tt(qdT[b][:], Q_[:, 0:4, :], EGB[b][:], ALU.mult), [Q_, EGB[b]], [qdT[b]])
                    k.op("pool", tt(vb[b][:], kvtok[b][:, 1], bc(sm["beta"][:, ci, dsl]), ALU.mult), [kvtok[b], sm["beta"]], [vb[b]])
                    k.op("pool", tt(kbg[b][:], kvtok[b][:, 0], bc(sm["bw"][:, ci, dsl]), ALU.mult), [kvtok[b], sm["bw"]], [kbg[b]])
                    k.op("pool", tt(kdec[b][:], kvtok[b][:, 0], bc(sm["kds"][:, ci, dsl]), ALU.mult), [kvtok[b], sm["kds"]], [kdec[b]])
                    yield
                    Xc, Xn = Xa[b], Xb[b]
                    ybuf, yap = YA[b], (lambda h, Y_=YA[b]: Y_[:, 0, h, :])
                    ynext = [Yb[b], XYalt[b]]
                    Qcur, Qnxt = Qa[b], Qc[b]
                    for lvl in range(6):
                        k.op("pe", [mm(A_[:, h, :], yap(h), Xc[:, h, :]) for h in range(4)], [Xc, ybuf], [A_])
                        if lvl < 5:
                            k.op("pe", [mm(B_[:, h, :], Xc[:, h, :], yap(h)) for h in range(4)], [Xc, ybuf], [B_])
                        k.op("act", acp(Xn[:], A_[:]), [A_], [Xn])
                        k.op("dve", tt(XI[b][:], A_[:], I4[:], ALU.add), [A_, I4], [XI[b]])
                        if lvl < 5:
                            Yn = ynext[lvl % 2]
                            k.op("dve", cp(Yn[:], B_[:]), [B_], [Yn])
                        yield
                        k.op("pe", [mm(C_[:, h, :], XI[b][:, h, :], Qcur[:, h, :]) for h in range(4)], [XI[b], Qcur], [C_])
                        k.op("act", acp(Qnxt[:], C_[:]), [C_], [Qnxt])
                        Qcur, Qnxt = Qnxt, Qcur
                        Xc, Xn = Xn, Xc
                        if lvl < 5:
                            ybuf, yap = Yn, (lambda h, Y_=Yn: Y_[:, h, :])
                        yield
                    Qfin[b] = Qcur
                    k.op("pe", [mm(A_[:, h, :], kbg[b][:, h, :], Qcur[:, h, :]) for h in range(4)], [kbg[b], Qcur], [A_])
                    k.op("act", act(nwcT[b][:], A_[:], AF.Copy, scale=-1.0), [A_], [nwcT[b]])
                    yield

                XYalt = dbl([128, 4, 128], BF16, "gYalt")

                def scan(ci, n):
                    t0, w = self.chunks[ci]
                    b = n % NB
                    Qb = Qfin[b]
                    k.op("pe", [f for h in range(4) for f in (mm(C_[:, h, :], Qb[:, h, :], vb[b][:, h, :], True, False),
                                                              mm(C_[:, h, :], nwcT[b][:, h, :], Sb[:, h, :], False, True))],
                         [Qb, vb[b], nwcT[b], Sb], [C_])
                    k.op("act", acp(vn[b][:], C_[:]), [C_], [vn[b]])
                    yield
                    k.op("pe", [f for h in range(4) for f in (mm(A_[:, h, :], Sb[:, h, :], qdT[b][:, h, :], True, False),
                                                              mm(A_[:, h, :], vn[b][:, h, :], YA[b][:, 1, h, :], False, True))],
                         [Sb, qdT[b], vn[b], YA[b]], [A_])
                    k.op("pe", [mm(B_[:, h, :], kdec[b][:, h, :], vn[b][:, h, :]) for h in range(4)], [kdec[b], vn[b]], [B_])
                    for h in range(4):
                        k.op("dve", stt(Sf[:, h, :], Sf[:, h, :], sm["egl"][:, ci, d * 4 + h:d * 4 + h + 1], B_[:, h, :], ALU.mult, ALU.add), [Sf, sm["egl"], B_], [Sf])
                    k.op("act", acp(Sb[:], Sf[:]), [Sf], [Sb])
                    k.op("act", acp(osb[b][:], A_[:]), [A_], [osb[b]])
                    k.dma("pool", self.OF.t[d, :, :, t0:t0 + w].rearrange("h p t -> p h t"), osb[b][:, :, 0:w], osb[b], self.OF)
                    yield

                def run():
                    order = list(range(NCH))
                    if d == 1:
                        order = order[::-1]
                    yield from prep(order[0], 0)
                    for n, ci in enumerate(order):
                        if n + 1 < len(order):
                            yield from prep(order[n + 1], n + 1)
                        yield from scan(ci, n)
                return run()

            gens = [make_dir(0), make_dir(1)]
            live = [True, True]
            while any(live):
                for i in range(2):
                    if live[i]:
                        try:
                            next(gens[i])
                        except StopIteration:
                            live[i] = False
            k.barrier([self.OF])

    def phase_b4(self):
        k, L = self.k, self.L
        k.push_phase()
        with k.phase:
            of = [k.sb([128, 2, 4, 512], F32, "b4of") for _ in range(2)]
            uz = [k.sb([128, 4, 512], F32, "b4uz") for _ in range(2)]
            o = k.sb([128, 4, 512], F32, "b4o")
            sq = [k.sb([128, 512], BF16, "b4sq") for _ in range(2)]
            tmp = k.sb([128, 4, 512], F32, "b4tmp")
            rs = k.sb([128, 4, 512], F32, "b4rs")
            zs = k.sb([128, 4, 512], F32, "b4zs")
            yb = [k.sb([128, 4, 512], BF16, "b4y") for _ in range(2)]
            dn = k.sb([128, 1], F32, "b4dn")
            psN = [k.ps([128, 512], F32, "b4ps") for _ in range(2)]
            k.dma("sp", dn[:], self.delta_norm[:, :], self.delta_norm, dn)
            tilesB = [(t0, min(512, L - t0)) for t0 in range(0, L, 512)]
            for ti, (t0, w) in enumerate(tilesB):
                OFt, UZt, Y = of[ti % 2], uz[ti % 2], yb[ti % 2]
                for dd_ in range(2):
                    k.dma("sp", OFt[:, dd_, :, 0:w], self.OF.t[dd_, :, :, t0:t0 + w].rearrange("h p t -> p h t"), self.OF, OFt)
                k.dma("sp", UZt[:, :, 0:w], self.UZ.t[:, :, t0:t0 + w].rearrange("h p t -> p h t"), self.UZ, UZt)
                k.op("pool", tt(o[:, :, 0:w], OFt[:, 0, :, 0:w], OFt[:, 1, :, 0:w], ALU.add), [OFt], [o])
                k.op("act", act(zs[:, :, 0:w], UZt[:, :, 0:w], AF.Silu), [UZt], [zs])
                for h in range(4):
                    s_, pn = sq[h % 2], psN[h % 2]
                    k.op("act", act(s_[:, 0:w], o[:, h, 0:w], AF.Square), [o], [s_])
                    k.op("pe", mm(pn[:, 0:w], self.onesb[:], s_[:, 0:w]), [s_, self.onesb], [pn])
                    k.op("act", act(tmp[:, h, 0:w], pn[:, 0:w], AF.Ln, bias=self.epsc[:, 0:1], scale=1.0 / 128), [pn, self.epsc], [tmp])
                k.op("act", act(rs[:, :, 0:w], tmp[:, :, 0:w], AF.Exp, scale=-0.5), [tmp], [rs])
                k.op("pool", tt(o[:, :, 0:w], o[:, :, 0:w], rs[:, :, 0:w], ALU.mult), [o, rs], [o])
                for h in range(4):
                    k.op("dve", stt(Y[:, h, 0:w], o[:, h, 0:w], dn[:, 0:1], zs[:, h, 0:w], ALU.mult, ALU.mult), [o, dn, zs], [Y])
                k.dma("pool", self.YM.t[4:8, :, t0:t0 + w].rearrange("g p t -> p g t"), Y[:, :, 0:w], Y, self.YM)
            k.barrier([self.YM])


    def phase_ce(self, slot, layer):
        k, L = self.k, self.L
        k.push_phase()
        with k.phase:
            H = [k.sb([128, 8, 512], F32, "ceH") for _ in range(2)]
            M = k.sb([128, 8, 512], BF16, "ceM")
            sq = [k.sb([128, 512], BF16, "cesq") for _ in range(2)]
            tmp = k.sb([128, 512], F32, "cetmp")
            rstd = k.sb([128, 512], F32, "cerstd")
            xn1 = [k.sb([128, 8, 512], BF16, "cexn") for _ in range(2)]
            h1 = [k.sb([128, 8, 512], BF16, "ceh1") for _ in range(4)]
            r = [k.sb([128, 512], F32, "cer") for _ in range(2)]
            wo = [k.sb([128, 1024], BF16, "cewo") for _ in range(4)]
            w1 = [k.sb([128, 2, 1024], BF16, "cew1") for _ in range(4)]
            w2 = [k.sb([128, 4096], BF16, "cew2") for _ in range(2)]
            psM = [k.ps([128, 512], F32, "cepsM") for _ in range(2)]
            psS = k.ps([128, 512], F32, "cepsS")
            psF = [k.ps([128, 512], F32, "cepsF") for _ in range(2 if layer == 0 else 3)]
            NF = len(psF)
            if layer == 0:
                psQ = [k.ps([128, 512], F32, "cepsQ") for _ in range(2)]
                xn2 = k.sb([128, 8, 512], BF16, "cexn2")
                wq = [k.sb([128, 3, 1024], BF16, "cewq") for _ in range(2)]
                rope = k.sb([128, 2, 512], F32, "cerope")
                qkn = k.sb([128, 2], F32, "ceqkn")
                permf = k.sb([128, 128], F32, "cepermf")
                permb = k.sb([128, 128], BF16, "cepermb")
                sqh = [k.sb([128, 512], BF16, "cesqh") for _ in range(2)]
                tmpq = [k.sb([128, 512], F32, "cetmpq") for _ in range(2)]
                qg = [k.sb([128, 512], F32, "ceqg") for _ in range(2)]
                qgb = [k.sb([128, 512], BF16, "ceqgb") for _ in range(2)]
                t1 = k.sb([128, 512], F32, "cet1")
                t2 = k.sb([128, 512], F32, "cet2")
                obq = k.sb([128, 8, 512], BF16, "ceobq")
                obk = k.sb([128, 2, 512], BF16, "ceobk")
                vb = k.sb([128, 4, 256], BF16, "cevb")
                psV = k.ps([128, 2, 256], F32, "cepsV")
                k.dma("sp", qkn[:], self.qk_norm[:, :], self.qk_norm, qkn)
                k.dma("sp", permf[:], self.c_perm[:, :], self.c_perm, permf)
                k.op("dve", cp(permb[:], permf[:]), [permf], [permb])
                Hsrc, Msrc, wt_out = self.H0, self.YM, self.wt_out_even
            else:
                yt = k.sb([128, 4, D], F32, "ceyt")
                Hsrc, Msrc, wt_out = self.H1, self.AT, self.wt_out_odd
            tiles = [t for ti, t in enumerate(self.tiles) if not (layer == 1 and ti == 0)]
            cnt = {"wo": 0, "w1": 0, "w2": 0, "wq": 0}

            def S1(i):
                t0, w = tiles[i]
                Hb, xn = H[i % 2], xn1[i % 2]
                k.dma("sp", Hb[:, :, 0:w], Hsrc.t[:, :, t0:t0 + w].rearrange("c p t -> p c t"), Hsrc, Hb)
                k.dma("sp", M[:, :, 0:w], Msrc.t[:, :, t0:t0 + w].rearrange("c p t -> p c t"), Msrc, M)
                for m in range(8):
                    W = wo[cnt["wo"] % 4]; cnt["wo"] += 1
                    k.dma("sp", W[:, :], wt_out.t[m, :, :], wt_out, W)
                    pm = psM[m % 2]
                    k.op("pe", [mm(pm[:, 0:w], W[:, c * 128:(c + 1) * 128], M[:, c, 0:w], c == 0, c == 7) for c in range(8)], [W, M], [pm])
                    k.op("dve", tt(Hb[:, m, 0:w], Hb[:, m, 0:w], pm[:, 0:w], ALU.add), [Hb, pm], [Hb])
                self.rms_to_bf16(Hb, w, sq, psS, tmp, rstd, xn)

            def S2(i):
                t0, w = tiles[i]
                xn = xn1[i % 2]
                for fp in range(16):
                    W = w1[cnt["w1"] % 4]; cnt["w1"] += 1
                    k.dma("sp", W[:, :, :], self.wt_mlp_in[layer].t[2 * fp:2 * fp + 2, :, :].rearrange("m p c -> p m c"), self.wt_mlp_in[layer], W)
                    for fi in range(2):
                        f = 2 * fp + fi
                        pf, rr, hg = psF[f % NF], r[f % 2], h1[f // 8]
                        k.op("pe", [mm(pf[:, 0:w], W[:, fi, c * 128:(c + 1) * 128], xn[:, c, 0:w], c == 0, c == 7) for c in range(8)], [W, xn], [pf])
                        k.op("act", act(rr[:, 0:w], pf[:, 0:w], AF.Relu), [pf], [rr])
                        k.op("pool", tt(hg[:, f % 8, 0:w], rr[:, 0:w], rr[:, 0:w], ALU.mult), [rr], [hg])
                    yield

            def S3a(i):
                t0, w = tiles[i]
                Hb = H[i % 2]
                for m in range(8):
                    W = w2[cnt["w2"] % 2]; cnt["w2"] += 1
                    k.dma("sp", W[:, :], self.wt_mlp_out[layer].t[m, :, :], self.wt_mlp_out[layer], W)
                    pm = psM[m % 2]
                    for fg in range(4):
                        k.op("pe", [mm(pm[:, 0:w], W[:, f * 128:(f + 1) * 128], h1[fg][:, f % 8, 0:w], f == 0, f == 31) for f in range(fg * 8, fg * 8 + 8)],
                             [W, h1[fg]], [pm])
                    k.op("dve", tt(Hb[:, m, 0:w], Hb[:, m, 0:w], pm[:, 0:w], ALU.add), [Hb, pm], [Hb])
                if layer == 0:
                    k.dma("pool", self.H1.t[:, :, t0:t0 + w].rearrange("c p t -> p c t"), Hb[:, :, 0:w], Hb, self.H1)
                    self.rms_to_bf16(Hb, w, sq, psS, tmp, rstd, xn2)
                else:
                    for j in range(4):
                        for cg in range(2):
                            pf = psF[(2 * j + cg) % 3]
                            k.op("pe", [tr(pf[:, ci * 128:(ci + 1) * 128], Hb[:, cg * 4 + ci, j * 128:(j + 1) * 128], self.ident[:]) for ci in range(4)], [Hb, self.ident], [pf])
                            k.op("act" if cg else "dve", (acp if cg else cp)(yt[:, j, cg * 512:(cg + 1) * 512], pf[:, :]), [pf], [yt])
                    r0 = t0 - NMETA
                    k.dma("pool", self.y[slot].t[r0:r0 + 512, :].rearrange("(j p) d -> p j d", p=128), yt[:, :, :], yt, self.y[slot])

            def S3b(i):
                t0, w = tiles[i]
                nsub = (w + 127) // 128
                pw = min(w, 128)
                xn = xn2
                k.dma("sp", rope[:, :, 0:w], self.c_rope.t[:, :, t0:t0 + w].rearrange("a p t -> p a t"), self.c_rope, rope)
                Wof = {}

                def wload(g):
                    W = wq[cnt["wq"] % 2]; cnt["wq"] += 1
                    k.dma("sp", W[:, :, :], self.wt_in_odd.t[3 * g:3 * g + 3, :, :].rearrange("m p c -> p m c"), self.wt_in_odd, W)
                    Wof[g] = W

                def P(m):
                    g, mi = divmod(m, 3)
                    if mi == 0:
                        wload(g)
                    W, pf, i2 = Wof[g], psQ[m % 2], m % 2
                    k.op("pe", [mm(pf[:, 0:w], W[:, mi, c * 128:(c + 1) * 128], xn[:, c, 0:w], c == 0, c == 7) for c in range(8)], [W, xn], [pf])
                    k.op("act", act(sqh[i2][:, 0:w], pf[:, 0:w], AF.Square), [pf], [sqh[i2]])

                def N(m):
                    pf, i2 = psQ[m % 2], m % 2
                    k.op("pe", mm(psS[:, 0:w], self.onesb[:], sqh[i2][:, 0:w]), [sqh[i2], self.onesb], [psS])
                    k.op("act", act(tmpq[i2][:, 0:w], psS[:, 0:w], AF.Ln, bias=self.epsc[:, 0:1], scale=1.0 / 128), [psS, self.epsc], [tmpq[i2]])
                    k.op("act", act(tmpq[i2][:, 0:w], tmpq[i2][:, 0:w], AF.Exp, scale=-0.5), [tmpq[i2]], [tmpq[i2]])
                    gc_ = qkn[:, 0:1] if m < 8 else qkn[:, 1:2]
                    k.op("dve", stt(qg[i2][:, 0:w], pf[:, 0:w], gc_, tmpq[i2][:, 0:w], ALU.mult, ALU.mult), [pf, qkn, tmpq[i2]], [qg[i2]])
                    k.op("pool", cp(qgb[i2][:, 0:w], qg[i2][:, 0:w]), [qg[i2]], [qgb[i2]])

                def R(m):
                    i2 = m % 2
                    pm = psM[m % 2]
                    k.op("pe", mm(pm[:, 0:w], permb[:], qgb[i2][:, 0:w]), [permb, qgb[i2]], [pm])
                    k.op("pool", tt(t1[:, 0:w], qg[i2][:, 0:w], rope[:, 0, 0:w], ALU.mult), [qg[i2], rope], [t1])
                    k.op("dve", tt(t2[:, 0:w], pm[:, 0:w], rope[:, 1, 0:w], ALU.mult), [pm, rope], [t2])
                    if m < 8:
                        k.op("pool", tt(obq[:, m, 0:w], t1[:, 0:w], t2[:, 0:w], ALU.add), [t1, t2], [obq])
                        if m == 7:
                            k.dma("pool", self.QT.t[:, :, t0:t0 + w].rearrange("g p t -> p g t"), obq[:, :, 0:w], obq, self.QT)
                    else:
                        k.op("pool", tt(obk[:, m - 8, 0:w], t1[:, 0:w], t2[:, 0:w], ALU.add), [t1, t2], [obk])
                        if m == 9:
                            k.dma("pool", self.KT.t[:, :, t0:t0 + w].rearrange("g p t -> p g t"), obk[:, :, 0:w], obk, self.KT)

                for m in range(10):
                    P(m)
                    yield
                    N(m)
                    yield
                    if m >= 1:
                        R(m - 1)
                        yield
                R(9)
                yield
                W = Wof[3]
                for jp in range(0, nsub, 2):
                    js = list(range(jp, min(jp + 2, nsub)))
                    k.op("pe", [mm(psV[0:pw, j % 2, half * 128:(half + 1) * 128], xn[:, c, j * 128:j * 128 + pw], W[:, 1 + half, c * 128:(c + 1) * 128], c == 0, c == 7)
                                for j in js for half in range(2) for c in range(8)], [W, xn], [psV])
                    k.op("act", acp(vb[0:pw, jp:jp + len(js), :], psV[0:pw, 0:len(js), :]), [psV], [vb])
                    yield
                k.dma("pool", self.VT.t[t0:t0 + w, :].rearrange("(j p) f -> p j f", p=pw), vb[0:pw, 0:nsub, :], vb, self.VT)

            def drive(ga, gb, ratio):
                la, lb = ga is not None, gb is not None
                while la or lb:
                    if la:
                        try:
                            next(ga)
                        except StopIteration:
                            la = False
                    for _ in range(ratio):
                        if lb:
                            try:
                                next(gb)
                            except StopIteration:
                                lb = False

            n = len(tiles)
            S1(0)
            for i in range(n):
                drive(S2(i), S3b(i - 1) if (layer == 0 and i > 0) else None, 2)
                if i + 1 < n:
                    S1(i + 1)
                S3a(i)
            if layer == 0:
                drive(None, S3b(n - 1), 1)
            outs = [self.H1, self.QT, self.KT, self.VT] if layer == 0 else [self.y[slot]]
            k.barrier(outs)

    def phase_d(self):
        k, L = self.k, self.L
        k.push_phase()
        with k.phase:
            nkb = len(self.chunks)
            KTs = k.sb([128, 2, L], BF16, "dK")
            Vs = k.sb([128, nkb, 256], BF16, "dV")
            QTt = [k.sb([128, 512], BF16, "dQ") for _ in range(2)]
            NP = 6
            Pt = [k.sb([128, 512], BF16, "dP") for _ in range(NP)]
            sa = [k.sb([128, 512], BF16, "dsa") for _ in range(2)]
            sb_ = [k.sb([128, 512], BF16, "dsb") for _ in range(2)]
            sc = [k.sb([128, 512], BF16, "dsc") for _ in range(2)]
            rden = k.sb([128, 512], F32, "drden")
            ob = [k.sb([128, 512], BF16, "dob") for _ in range(2)]
            NSB = 4
            psS = [k.ps([128, 512], F32, "dpsS") for _ in range(NSB)]
            psO = [k.ps([128, 512], F32, "dpsO") for _ in range(2)]
            psD = [k.ps([128, 512], F32, "dpsD") for _ in range(2)]
            nsplit = 4
            step = (L + nsplit - 1) // nsplit
            for i in range(nsplit):
                a, b_ = i * step, min(L, (i + 1) * step)
                k.dma("sp", KTs[:, :, a:b_], self.KT.t[:, :, a:b_].rearrange("g p t -> p g t"), self.KT, KTs)
            k.op("pool", mset(Vs[:, 0, :], 0.0), [], [Vs])
            k.dma("sp", Vs[0:NMETA, 0, :], self.VT.t[0:NMETA, :], self.VT, Vs)
            nreal = nkb - 1
            assert nreal % 4 == 0
            for b0 in range(0, nreal, 16):
                nb = min(16, nreal - b0)
                k.dma("sp", Vs[:, 1 + b0:1 + b0 + nb, :], self.VT.t[NMETA + b0 * 128:NMETA + (b0 + nb) * 128, :].rearrange("(b p) f -> p b f", p=128), self.VT, Vs)
            scale = 128.0 ** -0.5
            n = 0
            for h in range(8):
                kv = h // 4
                for ti, (t0, w) in enumerate(self.tiles):
                    Q, po, pd, O = QTt[n % 2], psO[n % 2], psD[n % 2], ob[n % 2]
                    k.dma("sp", Q[:, 0:w], self.QT.t[h, :, t0:t0 + w], self.QT, Q)

                    def smm(kb):
                        k0, kw = self.chunks[kb]
                        ps = psS[kb % NSB]
                        k.op("pe", mm(ps[0:kw, 0:w], KTs[:, kv, k0:k0 + kw], Q[:, 0:w]), [KTs, Q], [ps])

                    def odm(kb):
                        k0, kw = self.chunks[kb]
                        ps, P = psS[kb % NSB], Pt[kb % NP]
                        k.op("act", act(P[0:kw, 0:w], ps[0:kw, 0:w], AF.Exp, scale=scale), [ps], [P])
                        k.op("pe", mm(po[:, 0:w], Vs[0:kw, kb, kv * 128:(kv + 1) * 128], P[0:kw, 0:w], kb == 0, kb == nkb - 1), [Vs, P], [po])
                        if kb == 0:
                            k.op("pe", mm(pd[:, 0:w], self.onesb[0:kw, :], P[0:kw, 0:w], True, False), [P, self.onesb], [pd])
                            return
                        r_, qd = (kb - 1) % 4, ((kb - 1) // 4) % 2
                        if r_ == 1:
                            k.op("dve", tt(sa[qd][:, 0:w], Pt[(kb - 1) % NP][:, 0:w], P[:, 0:w], ALU.add), [Pt[(kb - 1) % NP], P], [sa[qd]])
                        elif r_ == 3:
                            k.op("pool", tt(sb_[qd][:, 0:w], Pt[(kb - 1) % NP][:, 0:w], P[:, 0:w], ALU.add), [Pt[(kb - 1) % NP], P], [sb_[qd]])
                            if pend:
                                flush()
                            k.op("dve", tt(sc[qd][:, 0:w], sa[qd][:, 0:w], sb_[qd][:, 0:w], ALU.add), [sa[qd], sb_[qd]], [sc[qd]])
                            pend.append((qd, kb == nkb - 1))
                    pend = []

                    def flush():
                        qd_, last_ = pend.pop(0)
                        k.op("pe", mm(pd[:, 0:w], self.onesb[:], sc[qd_][:, 0:w], False, last_), [sc[qd_], self.onesb], [pd])
                    smm(0)
                    smm(1)
                    for kb in range(nkb):
                        if kb + 2 < nkb:
                            smm(kb + 2)
                        odm(kb)
                    while pend:
                        flush()
                    k.op("act", act(rden[:, 0:w], pd[:, 0:w], AF.Ln), [pd], [rden])
                    k.op("act", act(rden[:, 0:w], rden[:, 0:w], AF.Exp, scale=-1.0), [rden], [rden])
                    k.op("dve", tt(O[:, 0:w], po[:, 0:w], rden[:, 0:w], ALU.mult), [po, rden], [O])
                    k.dma("pool", self.AT.t[h, :, t0:t0 + w], O[:, 0:w], O, self.AT)
                    n += 1
            k.barrier([self.AT])

    def build(self):
        k = self.k
        with self.stack:
            self.declare()
            self.load_consts()
            self.prologue()
            for slot in range(self.NSLOT):
                self.phase_a(slot)
                if self.stop_after == "a":
                    continue
                self.phase_b12()
                if self.stop_after == "b2":
                    continue
                self.phase_b3()
                self.phase_b4()
                if self.stop_after == "b4":
                    continue
                self.phase_ce(slot, 0)
                if self.stop_after == "c":
                    continue
                self.phase_d()
                if self.stop_after == "d":
                    continue
                self.phase_ce(slot, 1)
        return self.nc


def host_consts(S):
    L = S + NMETA
    ident = np.eye(128, dtype=np.float32)
    i = np.arange(128)[:, None]
    j = np.arange(128)[None, :]
    masks = np.zeros((128, 6, 128), np.float32)
    masks[:, 0, :] = np.where(i > j, 0.0, NEG)
    masks[:, 1, :] = np.where(i < j, 0.0, NEG)
    masks[:, 2, :] = (i <= j).astype(np.float32)
    masks[:, 3, :] = (i >= j).astype(np.float32)
    masks[:, 4, :] = (i < NMETA).astype(np.float32)
    perm = np.zeros((128, 128), np.float32)
    perm[(np.arange(128) + 64) % 128, np.arange(128)] = 1.0
    rows = S // 64
    row = np.repeat(np.arange(rows), 64).astype(np.float32)
    col = np.tile(np.arange(64), rows).astype(np.float32)
    freqs = (10000.0 ** (-(np.arange(32, dtype=np.float32) / 32))).astype(np.float32)
    ang = np.concatenate([row[:, None] * freqs, col[:, None] * freqs], axis=-1)
    ang = np.concatenate([np.zeros((NMETA, 64), np.float32), ang], axis=0)
    cos, sin = np.cos(ang).T, np.sin(ang).T
    rope = np.zeros((2, 128, L), np.float32)
    rope[0, :64], rope[0, 64:] = cos, cos
    rope[1, :64], rope[1, 64:] = -sin, sin
    return dict(c_ident=ident, c_masks=masks, c_perm=perm, c_rope=rope)


def host_weights(p):
    f = lambda a: np.ascontiguousarray(np.asarray(a, dtype=np.float32))
    out = {}
    out["meta"] = f(p["meta_tokens"])
    out["g_mix"] = f(np.asarray(p["mix_norm"]).reshape(2, 8, 128).transpose(2, 0, 1))
    out["g_mlp"] = f(np.asarray(p["mlp_norm"]).reshape(2, 8, 128).transpose(2, 0, 1))
    out["w_in_even"] = f(p["w_in_even"][0])
    out["pool_w"] = f(p["pool_w"][0])
    out["pool_scale"] = f(np.asarray(p["pool_scale"][0]).reshape(4, 128).T)
    out["conv_w"] = f(np.asarray(p["conv_qkv"][0]).reshape(7, 12, 128).transpose(2, 1, 0))
    out["a_log"] = f(np.broadcast_to(np.asarray(p["a_log"][0]).reshape(1, 8), (128, 8)))
    out["dt_bias"] = f(np.broadcast_to(np.asarray(p["dt_bias"][0]).reshape(1, 8), (128, 8)))
    out["delta_norm"] = f(np.asarray(p["delta_norm"][0]).reshape(128, 1))
    out["w_out_even"] = f(p["w_out_even"][0])
    out["w_in_odd"] = f(p["w_in_odd"][0])
    out["qk_norm"] = f(np.stack([np.asarray(p["q_norm"][0]), np.asarray(p["k_norm"][0])], axis=1))
    out["w_out_odd"] = f(p["w_out_odd"][0])
    out["w_mlp_in"] = f(p["w_mlp_in"])
    out["w_mlp_out"] = f(p["w_mlp_out"])
    return out


_CACHE = {}


def run(seqs, params, S, n_cores, debug=False, stop_after=None, trace=False):
    nslot = (len(seqs) + n_cores - 1) // n_cores
    prog = Prog(S, nslot, debug=debug, stop_after=stop_after)
    nc = prog.build()
    base = dict(host_weights(params))
    base.update(host_consts(S))
    in_maps = []
    zero = np.zeros((S, D), np.float32)
    for c in range(n_cores):
        m = dict(base)
        for s in range(nslot):
            i = s * n_cores + c
            m["x%d" % s] = np.ascontiguousarray(seqs[i], dtype=np.float32) if i < len(seqs) else zero
        in_maps.append(m)
    res = run_bass_kernel_spmd(nc, in_maps, core_ids=list(range(n_cores)), trace=trace)
    return res, nslot, prog


def kernel(x_prompt, x_sample, **params):
    x_prompt = np.asarray(x_prompt)
    x_sample = np.asarray(x_sample)
    S = x_prompt.shape[1]
    seqs = [x_prompt[i] for i in range(x_prompt.shape[0])] + [x_sample[i] for i in range(x_sample.shape[0])]
    n_cores = 8
    res, nslot, _ = run(seqs, params, S, n_cores)
    outs = []
    for i in range(len(seqs)):
        c, s = i % n_cores, i // n_cores
        outs.append(np.asarray(res.results[c]["y%d" % s], dtype=np.float32))
    nb = x_prompt.shape[0]
    return np.stack(outs[:nb]), np.stack(outs[nb:])
```
